# Optimizing a Trainium2 kernel written in Bass

```python
import math
import jax, jax.numpy as jnp
from jax import lax
import numpy as np

D_MODEL = 2048
BATCH = 4
SEQ = 2048
DEPTH = 4

CHUNK = 64
Q_BLOCK = 128
ROPE_THETA = 500000.0
MLA_HEADS = 8
Q_LORA = 512
KV_LORA = 256
QK_NOPE = 128
QK_ROPE = 64
V_HEAD = 128
MLA_QK_DIM = QK_NOPE + QK_ROPE
DIFF_HEADS = 8
DIFF_D = 64
DIFF_ROT = DIFF_D // 4
DIFF_QK_COLS = DIFF_HEADS * 2 * DIFF_D
DIFF_V_COLS = DIFF_HEADS * 2 * DIFF_D
MLA_IN_COLS = Q_LORA + KV_LORA + QK_ROPE
IN_COLS = MLA_IN_COLS + 2 * DIFF_QK_COLS + DIFF_V_COLS
MIX_WIDTH = MLA_HEADS * V_HEAD + DIFF_HEADS * 2 * DIFF_D
D_FF = 5632
CONV_WIDTH = 3
PLE_DIM = 256
RMS_EPS = 1e-6

kernel_name = "hybrid_mla_diffattn_convglu_trunk"


def rms_norm(x, g):
    xf = x.astype(jnp.float32)
    y = xf * lax.rsqrt(jnp.mean(xf * xf, axis=-1, keepdims=True) + RMS_EPS)
    return (y * g.astype(jnp.float32)).astype(x.dtype)


def rope_tables(positions, rot_dim):
    inv = ROPE_THETA ** (-jnp.arange(0, rot_dim, 2, dtype=jnp.float32) / rot_dim)
    ang = positions.astype(jnp.float32)[..., None] * inv
    return jnp.cos(ang)[:, :, None, :], jnp.sin(ang)[:, :, None, :]


def apply_rope(x, cos, sin):
    half = cos.shape[-1]
    rot = 2 * half
    xf = x[..., :rot].astype(jnp.float32)
    x1, x2 = xf[..., :half], xf[..., half:]
    r = jnp.concatenate([x1 * cos - x2 * sin, x2 * cos + x1 * sin], axis=-1).astype(x.dtype)
    return jnp.concatenate([r, x[..., rot:]], axis=-1)


def chunk_causal_attention(q, k, v, scale):
    S = q.shape[1]
    outs = []
    for start in range(0, S, Q_BLOCK):
        end = start + Q_BLOCK
        s = jnp.einsum('bqhd,bkhd->bhqk', q[:, start:end], k[:, :end]).astype(jnp.float32) * scale
        q_chunk = (start + jnp.arange(Q_BLOCK)) // CHUNK
        k_chunk = jnp.arange(end) // CHUNK
        mask = k_chunk[None, :] <= q_chunk[:, None]
        s = jnp.where(mask[None, None], s, -1e30)
        pr = jax.nn.softmax(s, axis=-1).astype(v.dtype)
        outs.append(jnp.einsum('bhqk,bkhd->bqhd', pr, v[:, :end]))
    return jnp.concatenate(outs, axis=1)


def mla_mixer(c_q, c_kv, k_pe, g_q, w_uq, g_kv, w_ukv, cos, sin):
    B, S, _ = c_q.shape
    q = (rms_norm(c_q, g_q) @ w_uq).reshape(B, S, MLA_HEADS, MLA_QK_DIM)
    q = jnp.concatenate([q[..., :QK_NOPE], apply_rope(q[..., QK_NOPE:], cos, sin)], axis=-1)
    kv = (rms_norm(c_kv, g_kv) @ w_ukv).reshape(B, S, MLA_HEADS, QK_NOPE + V_HEAD)
    k_nope, v = kv[..., :QK_NOPE], kv[..., QK_NOPE:]
    k_rope = apply_rope(k_pe[:, :, None, :], cos, sin)
    k = jnp.concatenate([k_nope, jnp.broadcast_to(k_rope, (B, S, MLA_HEADS, QK_ROPE))], axis=-1)
    o = chunk_causal_attention(q, k, v, MLA_QK_DIM ** -0.5)
    return o.reshape(B, S, MLA_HEADS * V_HEAD)


def diff_mixer(dq, dk, dv, lq1, lk1, lq2, lk2, g_sub, lambda_init, cos, sin):
    B, S, _ = dq.shape
    q = apply_rope(dq.reshape(B, S, 2 * DIFF_HEADS, DIFF_D), cos, sin).reshape(B, S, DIFF_HEADS, 2, DIFF_D)
    k = apply_rope(dk.reshape(B, S, 2 * DIFF_HEADS, DIFF_D), cos, sin).reshape(B, S, DIFF_HEADS, 2, DIFF_D)
    v = dv.reshape(B, S, DIFF_HEADS, 2 * DIFF_D)
    lam = (jnp.exp(jnp.sum(lq1.astype(jnp.float32) * lk1.astype(jnp.float32)))
           - jnp.exp(jnp.sum(lq2.astype(jnp.float32) * lk2.astype(jnp.float32))) + lambda_init)
    scale = DIFF_D ** -0.5
    a1 = chunk_causal_attention(q[..., 0, :], k[..., 0, :], v, scale)
    a2 = chunk_causal_attention(q[..., 1, :], k[..., 1, :], v, scale)
    o = a1 - lam.astype(a1.dtype) * a2
    o = rms_norm(o, g_sub) * (1.0 - lambda_init)
    return o.reshape(B, S, DIFF_HEADS * 2 * DIFF_D)


def causal_dwconv(u, w, b):
    K, C = w.shape
    y = lax.conv_general_dilated(u, w[:, None, :].astype(u.dtype), window_strides=(1,),
                                 padding=[(K - 1, 0)], dimension_numbers=('NWC', 'WIO', 'NWC'),
                                 feature_group_count=C)
    return y + b


def conv_geglu_ffn(h, w_up, conv_w, conv_b, w_down):
    u = causal_dwconv(h @ w_up, conv_w, conv_b)
    gate, up = u[..., :D_FF], u[..., D_FF:]
    return (jax.nn.gelu(gate, approximate=True) * up) @ w_down


def setup_inputs(seed: int = 0) -> dict:
    key = jax.random.key(seed)
    ks = jax.random.split(key, 32)
    f32 = jnp.float32

    def w(k, shape, fan_in):
        return jax.random.normal(k, shape, f32) * (fan_in ** -0.5)

    def gain(k, shape):
        return 1.0 + 0.05 * jax.random.normal(k, shape, f32)

    offsets = jax.random.randint(ks[2], (BATCH,), 0, 64, dtype=jnp.int32) * CHUNK
    positions = offsets[:, None] + jnp.arange(SEQ, dtype=jnp.int32)[None, :]
    return {
        "x": jax.random.normal(ks[0], (BATCH, SEQ, D_MODEL), f32),
        "p": jax.random.normal(ks[1], (DEPTH, BATCH, SEQ, PLE_DIM), f32),
        "positions": positions,
        "g_mix_pre": gain(ks[3], (DEPTH, D_MODEL)),
        "w_in": w(ks[4], (DEPTH, D_MODEL, IN_COLS), D_MODEL),
        "g_q_lora": gain(ks[5], (DEPTH, Q_LORA)),
        "w_uq": w(ks[6], (DEPTH, Q_LORA, MLA_HEADS * MLA_QK_DIM), Q_LORA),
        "g_kv_lora": gain(ks[7], (DEPTH, KV_LORA)),
        "w_ukv": w(ks[8], (DEPTH, KV_LORA, MLA_HEADS * (QK_NOPE + V_HEAD)), KV_LORA),
        "lambda_q1": 0.1 * jax.random.normal(ks[9], (DEPTH, DIFF_D), f32),
        "lambda_k1": 0.1 * jax.random.normal(ks[10], (DEPTH, DIFF_D), f32),
        "lambda_q2": 0.1 * jax.random.normal(ks[11], (DEPTH, DIFF_D), f32),
        "lambda_k2": 0.1 * jax.random.normal(ks[12], (DEPTH, DIFF_D), f32),
        "g_diff_sub": gain(ks[13], (DEPTH, 2 * DIFF_D)),
        "w_o": w(ks[14], (DEPTH, MIX_WIDTH, D_MODEL), MIX_WIDTH),
        "g_mix_post": gain(ks[15], (DEPTH, D_MODEL)),
        "g_ffn_pre": gain(ks[16], (DEPTH, D_MODEL)),
        "w_up": w(ks[17], (DEPTH, D_MODEL, 2 * D_FF), D_MODEL),
        "conv_w": w(ks[18], (DEPTH, CONV_WIDTH, 2 * D_FF), CONV_WIDTH),
        "conv_b": 0.02 * jax.random.normal(ks[19], (DEPTH, 2 * D_FF), f32),
        "w_down": w(ks[20], (DEPTH, D_FF, D_MODEL), D_FF),
        "g_ffn_post": gain(ks[21], (DEPTH, D_MODEL)),
        "w_ple": w(ks[22], (DEPTH, PLE_DIM, D_MODEL), PLE_DIM),
        "w_ple_gate": w(ks[23], (DEPTH, D_MODEL, D_MODEL), D_MODEL),
        "g_ple": gain(ks[24], (DEPTH, D_MODEL)),
    }


def reference(x, p, positions, g_mix_pre, w_in, g_q_lora, w_uq, g_kv_lora, w_ukv,
              lambda_q1, lambda_k1, lambda_q2, lambda_k2, g_diff_sub, w_o, g_mix_post,
              g_ffn_pre, w_up, conv_w, conv_b, w_down, g_ffn_post, w_ple, w_ple_gate, g_ple):
    cos_mla, sin_mla = rope_tables(positions, QK_ROPE)
    cos_dif, sin_dif = rope_tables(positions, DIFF_ROT)
    cos_mla, sin_mla = cos_mla.astype(x.dtype), sin_mla.astype(x.dtype)
    cos_dif, sin_dif = cos_dif.astype(x.dtype), sin_dif.astype(x.dtype)
    split_at = [Q_LORA, Q_LORA + KV_LORA, MLA_IN_COLS, MLA_IN_COLS + DIFF_QK_COLS,
                MLA_IN_COLS + 2 * DIFF_QK_COLS]
    h = x
    for i in range(DEPTH):
        lambda_init = 0.8 - 0.6 * math.exp(-0.3 * i)
        z = rms_norm(h, g_mix_pre[i]) @ w_in[i]
        c_q, c_kv, k_pe, dq, dk, dv = jnp.split(z, split_at, axis=-1)
        mla_out = mla_mixer(c_q, c_kv, k_pe, g_q_lora[i], w_uq[i], g_kv_lora[i], w_ukv[i],
                            cos_mla, sin_mla)
        diff_out = diff_mixer(dq, dk, dv, lambda_q1[i], lambda_k1[i], lambda_q2[i], lambda_k2[i],
                              g_diff_sub[i], lambda_init, cos_dif, sin_dif)
        mix = jnp.concatenate([mla_out, diff_out], axis=-1) @ w_o[i]
        h = h + rms_norm(mix, g_mix_post[i])
        f = conv_geglu_ffn(rms_norm(h, g_ffn_pre[i]), w_up[i], conv_w[i], conv_b[i], w_down[i])
        h = h + rms_norm(f, g_ffn_post[i])
        ple = (p[i] @ w_ple[i]) * jax.nn.sigmoid(h @ w_ple_gate[i])
        h = h + rms_norm(ple, g_ple[i])
    return h
```

```python
import math
from contextlib import ExitStack

import numpy as np
import ml_dtypes

import concourse.bass as bass
import concourse.mybir as mybir
from concourse.bass_utils import run_bass_kernel_spmd

F32 = mybir.dt.float32
BF16 = mybir.dt.bfloat16
I32 = mybir.dt.int32
AF = mybir.ActivationFunctionType
ALU = mybir.AluOpType

D = 2048
B = 4
S = 2048
DEPTH = 4
TPC = 1024
TT = 512
NCH = 16
DFF = 5632
NFF = 44
EPS = 1e-6
THETA = 500000.0
TWO_PI_HI = 6.28125
TWO_PI_LO = 2.0 * math.pi - 6.28125
NPBF = ml_dtypes.bfloat16


class Prog:
    ENG = ("pe", "act", "dve", "pool", "sp")

    def __init__(self):
        self.nc = bass.Bass("TRN2", target_bir_lowering=False)
        self.es = ExitStack()
        nc = self.nc
        self.q = {e: [] for e in self.ENG}
        self.cnt = {e: 0 for e in self.ENG}
        self.semobj = {}
        for e in ("pe", "act", "dve", "pool"):
            self.semobj[e] = self.es.enter_context(nc.semaphore(f"s_{e}"))
        self.dring = {}
        self.dtot = {}
        self.dnext = {}
        for qn, n in (("sp", 12), ("pool", 8)):
            keys = []
            for i in range(n):
                k = f"d_{qn}_{i}"
                self.semobj[k] = self.es.enter_context(nc.semaphore(k))
                self.dtot[k] = 0
                keys.append(k)
            self.dring[qn] = keys
            self.dnext[qn] = 0
        self.semobj["cc"] = self.es.enter_context(nc.semaphore("s_cc"))
        self.ccn = 0
        self.pending = {e: {} for e in self.ENG}
        self.known = {e: {} for e in self.ENG}
        self.lastw = {}
        self.readers = {}
        self.out_tokens = {}
        self.ps = [self.es.enter_context(nc.psum_tensor(f"ps{i}", [128, 512], F32)) for i in range(8)]
        self.rot = {"main": [0, 1, 2, 3], "aux": [4, 5], "stat": [6, 7]}
        self.roti = {k: 0 for k in self.rot}
        self.dram_in = {}
        self.dram_out = {}

    def inp(self, name, shape, dt=F32):
        t = self.nc.dram_tensor(name, list(shape), dt, kind="ExternalInput").ap()
        self.dram_in[name] = t
        return t

    def outp(self, name, shape, dt=F32):
        t = self.nc.dram_tensor(name, list(shape), dt, kind="ExternalOutput").ap()
        self.dram_out[name] = t
        return t

    def sb(self, name, shape, dt):
        return self.es.enter_context(self.nc.sbuf_tensor(name, list(shape), dt))

    def bank(self, kind="main"):
        r = self.rot[kind]
        i = self.roti[kind]
        self.roti[kind] = (i + 1) % len(r)
        return r[i]

    def _deps(self, eng, reads, writes, extra=()):
        deps = {}

        def add(tok):
            if tok is None:
                return
            k, v = tok
            if deps.get(k, 0) < v:
                deps[k] = v

        for r in reads:
            add(self.lastw.get(r))
            if isinstance(r, tuple) and r[0] == "ps":
                for k, v in self.readers.get(r, {}).items():
                    if k != eng:
                        add((k, v))
        for w in writes:
            add(self.lastw.get(w))
            for k, v in self.readers.get(w, {}).items():
                add((k, v))
        for t in extra:
            add(t)
        for k, v in self.pending[eng].items():
            add((k, v))
        self.pending[eng] = {}
        waits = []
        for k, v in deps.items():
            if eng == "pe" and k == "pe":
                continue
            if self.known[eng].get(k, 0) >= v:
                continue
            self.known[eng][k] = v
            waits.append((k, v))
        return waits

    def _commit(self, tok, reads, writes):
        for w in writes:
            self.lastw[w] = tok
            self.readers[w] = {}
        for r in reads:
            d = self.readers.setdefault(r, {})
            if d.get(tok[0], 0) < tok[1]:
                d[tok[0]] = tok[1]

    def op(self, eng, fn, reads=(), writes=()):
        waits = self._deps(eng, reads, writes)
        self.cnt[eng] += 1
        tok = (eng, self.cnt[eng])
        self.q[eng].append((waits, fn, True))
        self._commit(tok, reads, writes)
        return tok

    def dma(self, qn, pairs, reads=(), writes=(), is_out=False):
        ring = self.dring[qn]
        i = self.dnext[qn]
        self.dnext[qn] = (i + 1) % len(ring)
        k = ring[i]
        prev = self.dtot[k]
        extra = [(k, prev)] if prev > 0 else []
        waits = self._deps(qn, reads, writes, extra)
        total = prev + 16 * len(pairs)
        self.dtot[k] = total
        sem = self.semobj[k]

        def fn(e, pairs=pairs, sem=sem):
            for (o, a) in pairs:
                e.dma_start(out=o, in_=a).then_inc(sem, 16)
            return None

        self.q[qn].append((waits, fn, False))
        tok = (k, total)
        self._commit(tok, reads, writes)
        if is_out:
            self.out_tokens[k] = max(self.out_tokens.get(k, 0), total)
        return tok

    def barrier(self):
        toks = {e: self.cnt[e] for e in ("pe", "act", "dve", "pool") if self.cnt[e] > 0}
        for k, v in self.dtot.items():
            if v > 0:
                toks[k] = v
        for e in self.ENG:
            for k, v in toks.items():
                if self.pending[e].get(k, 0) < v:
                    self.pending[e][k] = v

    def collective(self, t_in, t_out, reads=(), writes=()):
        waits = self._deps("pool", reads, writes)
        self.ccn += 1
        sem = self.semobj["cc"]

        def fn(e, t_in=t_in, t_out=t_out, sem=sem):
            e.collective_compute("AllGather", ALU.bypass, replica_groups=RG, ins=[t_in.ap().opt()],
                                 outs=[t_out.ap().opt()]).then_inc(sem)
            return None

        self.q["pool"].append((waits, fn, False))
        tok = ("cc", self.ccn)
        self._commit(tok, reads, writes)
        return tok

    def finish(self):
        fw = [(k, v) for k, v in self.out_tokens.items()]
        self.q["sp"].append((fw, None, False))
        nc = self.nc
        emap = {"pe": "tensor", "act": "scalar", "dve": "vector", "pool": "gpsimd", "sp": "sync"}

        def emit(ename, e):
            for waits, fn, inc in self.q[ename]:
                for (k, v) in waits:
                    e.wait_ge(self.semobj[k], v)
                if fn is None:
                    continue
                inst = fn(e)
                if inc:
                    inst.then_inc(self.semobj[ename], 1)

        with nc.Block() as block:
            @block.tensor
            def _(e):
                emit("pe", e)

            @block.scalar
            def _(e):
                emit("act", e)

            @block.vector
            def _(e):
                emit("dve", e)

            @block.gpsimd
            def _(e):
                emit("pool", e)

            @block.sync
            def _(e):
                emit("sp", e)
        self.es.close()
        return nc


class Ctx:
    pass


def mm_group(P, bank_ap, pieces, reads, bank, extra_writes=()):
    n = len(pieces)

    def fn(e, pieces=pieces, bank_ap=bank_ap):
        inst = None
        for i, (l, r) in enumerate(pieces):
            inst = e.matmul(bank_ap, l, r, start=(i == 0), stop=(i == n - 1))
        return inst

    return P.op("pe", fn, reads=reads, writes=[("ps", bank)] + list(extra_writes))


def act(P, out, in_, func, reads, writes, **kw):
    return P.op("act", lambda e: e.activation(out=out, in_=in_, func=func, **kw), reads=reads, writes=writes)


def tt_op(P, eng, out, in0, in1, op, reads, writes):
    return P.op(eng, lambda e: e.tensor_tensor(out=out, in0=in0, in1=in1, op=op), reads=reads, writes=writes)


def stt(P, out, in0, scalar, in1, op0, op1, reads, writes):
    return P.op("dve", lambda e: e.scalar_tensor_tensor(out=out, in0=in0, scalar=scalar, in1=in1, op0=op0, op1=op1),
                reads=reads, writes=writes)


def ts(P, eng, out, in0, s1, s2, op0, op1, reads, writes):
    if s2 is None:
        return P.op(eng, lambda e: e.tensor_scalar(out=out, in0=in0, scalar1=s1, scalar2=None, op0=op0),
                    reads=reads, writes=writes)
    return P.op(eng, lambda e: e.tensor_scalar(out=out, in0=in0, scalar1=s1, scalar2=s2, op0=op0, op1=op1),
                reads=reads, writes=writes)


def rms_rstd(P, C, chunks, n, dim, rstd_ap, rstd_res):
    bank = P.bank("stat")
    nchk = len(chunks)
    for i, (ap, rd) in enumerate(chunks):
        j = C.sq_i
        C.sq_i = (C.sq_i + 1) % len(C.SQ)
        sq = C.SQ[j][:, :n]
        act(P, sq, ap, AF.Square, reads=rd, writes=[("sq", j)])
        P.op("pe", lambda e, sq=sq, i=i: e.matmul(P.ps[bank][:, :n], C.ONES[:, :], sq, start=(i == 0), stop=(i == nchk - 1)),
             reads=[("sq", j), "ones"], writes=[("ps", bank)])
    act(P, C.SQT[:, :n], P.ps[bank][:, :n], AF.Sqrt, reads=[("ps", bank), "epsv"], writes=["sqt"], bias=C.EPSV[:, 0:1], scale=1.0 / dim)
    P.op("dve", lambda e: e.reciprocal(out=rstd_ap, in_=C.SQT[:, :n]), reads=["sqt"], writes=[rstd_res])


WSLOT = 2816
NWS = 4
RG = [[0, 1], [2, 3], [4, 5], [6, 7]]
KV_ROWS = 4224
R_KN, R_KR, R_DK, R_VM, R_VD = 0, 1024, 1152, 2176, 3200
MLA_SCALE = 192 ** -0.5
DIFF_SCALE = 64 ** -0.5


def build_fused(L=DEPTH):
    P = Prog()
    nc = P.nc
    C = Ctx()
    C.sq_i = 0
    H = P.sb("H", [128, NCH, TPC], F32)
    R1 = P.sb("R1", [128, 8192], F32)
    R2 = P.sb("R2", [128, 11264], F32)
    ACTB = P.sb("ACTB", [128, NCH, TT], BF16)
    WS = P.sb("WS", [128, NWS, WSLOT], BF16)
    C.ONES = P.sb("ONES", [128, 128], BF16)
    C.SQ = [P.sb(f"SQ{i}", [128, TT], BF16) for i in range(2)]
    C.SQT = P.sb("SQT", [128, TT], F32)
    C.EPSV = P.sb("EPSV", [128, 1], F32)
    RSTD = P.sb("RSTD", [128, TT], F32)
    TMP = [R2[:, 1024 + i * 512:1536 + i * 512] for i in range(2)]
    GV = P.sb("GV", [128, L, 8, NCH], F32)
    CW = P.sb("CW", [128, 2 * NFF, 3], F32)
    CB = P.sb("CB", [128, 2 * NFF], F32)
    HNH = [P.sb(f"HNH{i}", [128, NCH, 2], BF16) for i in range(2)]
    HNL = P.sb("HNL", [128, NCH, 2], BF16)
    HNR = P.sb("HNR", [128, NCH, 2], BF16)
    RSTDH = P.sb("RSTDH", [128, 2], F32)
    SG = R2[:, 512:1024]
    INVF = P.sb("INVF", [128, 2], F32)
    PM = P.sb("PM", [128, 2, 128], BF16)
    CMASK = P.sb("CMASK", [128, 2], F32)
    LAMV = P.sb("LAMV", [128, 4, 64], F32)
    MISC = P.sb("MISC", [128, L, 4], F32)
    LT = P.sb("LT", [128, 64], F32)
    LS = P.sb("LS", [128, 4], F32)
    NEGLAM = P.sb("NEGLAM", [128, 1], F32)
    GS = P.sb("GS", [128, 1], F32)
    F32T = R1[:, :].rearrange("p (c t) -> p c t", c=NCH)
    TAB = P.sb("TAB", [128, 4, TPC], F32)

    hT_in = P.inp("hT", [128, NCH, TPC])
    gv_in = P.inp("gvec", [128, L, 8, NCH])
    pos_in = P.inp("pos", [128, TPC], I32)
    inv_in = P.inp("invf", [128, 2])
    pm_in = P.inp("pm", [2, 128, 128])
    cm_in = P.inp("cmask", [128, 2])
    lam_in = P.inp("lamv", [128, L, 4, 64])
    misc_in = P.inp("misc", [128, L, 4])
    w_in = P.inp("w_in_fm", [L, 23, 128, NCH * 128])
    w_dv = P.inp("w_in_dv", [L, 2, 128, NCH * 512])
    w_uq = P.inp("w_uq", [L, 12, 128, 4 * 128])
    w_kn = P.inp("w_ukv_k", [L, 8, 128, 2 * 128])
    w_vv = P.inp("w_ukv_v", [L, 128, 2 * 1024])
    w_o = P.inp("w_o", [L, NCH, 128, NCH * 128])
    w_up = P.inp("w_up", [L, 2 * NFF, 128, NCH * 128])
    cw_in = P.inp("conv_w", [L, 128, 2 * NFF, 3])
    cb_in = P.inp("conv_b", [L, 128, 2 * NFF])
    w_dn = P.inp("w_down", [L, NCH, 128, NFF * 128])
    w_pg = P.inp("w_ple_gate", [L, NCH, 128, NCH * 128])
    w_pl = P.inp("w_ple", [L, NCH, 128, 2 * 128])
    pT_in = P.inp("pT", [L, 128, 2, TPC])
    hT_out = P.outp("hT_out", [128, NCH, TPC])
    q_qn = nc.dram_tensor("s_qn", [8, 128, TPC], BF16).ap()
    q_qr = nc.dram_tensor("s_qr", [4, 128, TPC], BF16).ap()
    q_dq = nc.dram_tensor("s_dq", [8, 128, TPC], BF16).ap()
    KV_GROUPS = (("kr", 128), ("kn", 1024), ("vm", 1024), ("dk", 1024), ("vd", 1024))
    KVLT = {n: nc.dram_tensor(f"s_{n}_loc", [r, TPC], BF16) for n, r in KV_GROUPS}
    KVAT = {n: nc.dram_tensor(f"s_{n}_all", [2 * r, TPC], BF16) for n, r in KV_GROUPS}
    KVL = {n: t.ap() for n, t in KVLT.items()}
    KVA = {n: t.ap() for n, t in KVAT.items()}
    ot_loc = nc.dram_tensor("s_ot", [NCH, 128, TPC], BF16).ap()
    hl_loc_t = nc.dram_tensor("s_hloc", [128, 2 * NCH], BF16)
    hl_all_t = nc.dram_tensor("s_hall", [256, 2 * NCH], BF16)
    hl_loc, hl_all = hl_loc_t.ap(), hl_all_t.ap()

    ws_i = [0]

    def wload(src_ap, nelem, nslots=1):
        s = ws_i[0]
        if s + nslots > NWS:
            s = 0
        ws_i[0] = (s + nslots) % NWS
        res = [("w", s + k) for k in range(nslots)]
        dst = WS[:, s:s + nslots, :].rearrange("p s e -> p (s e)")[:, :nelem]
        ch = max(d for d in range(1, 2049) if nelem % d == 0)
        if ch != nelem:
            dst = dst.rearrange("p (a b) -> p a b", b=ch)
            src_ap = src_ap.rearrange("p (a b) -> p a b", b=ch)
        P.dma("pool", [(dst, src_ap)], reads=[], writes=res)
        return s, res

    def wslot(s, nslots=1):
        return WS[:, s:s + nslots, :].rearrange("p s e -> p (s e)")

    P.op("pool", lambda e: e.memset(C.ONES[:, :], 1.0), writes=["ones"])
    P.op("pool", lambda e: e.memset(C.EPSV[:, :], EPS), writes=["epsv"])
    P.dma("sp", [(GV[:, :, :, :], gv_in), (INVF[:, :], inv_in), (CMASK[:, :], cm_in),
                 (MISC[:, :, :], misc_in)], writes=["gvec", "invf", "cmask", "misc"])
    P.dma("pool", [(PM[:, :, :], pm_in.rearrange("k p m -> p k m"))], writes=["pm"])
    for tt in range(2):
        P.dma("sp", [(H[:, :, tt * TT:(tt + 1) * TT], hT_in[:, :, tt * TT:(tt + 1) * TT])],
              writes=[("H", c, tt) for c in range(NCH)])

    def Hc(c, tt):
        return H[:, c, tt * TT:(tt + 1) * TT]

    actb_all = [("actb", c) for c in range(NCH)]
    abuf_all = [("abuf", c) for c in range(NFF)]

    def linear(w_tiles, KC, in_fn, in_reads, evac, M=128, n=TT):
        deferred = None
        for oc, w in enumerate(w_tiles):
            nsl = (KC * M + WSLOT - 1) // WSLOT
            s, res = wload(w, KC * M, nsl)
            wv = wslot(s, nsl)
            bank = P.bank("main")
            mm_group(P, P.ps[bank][:M, :n], [(wv[:, kc * M:(kc + 1) * M], in_fn(kc)) for kc in range(KC)],
                     reads=res + in_reads, bank=bank)
            if deferred is not None:
                deferred()
            deferred = evac(oc, bank, s, res, wv)
        if deferred is not None:
            deferred()

    ANG = R2[:, 4096:5120]
    RED = R2[:, 5120:6144]
    KI = R2[:, 6144:7168].bitcast(I32)
    KF = R2[:, 7168:8192]
    MSK = R2[:, 8192:9216]
    POSI = R2[:, 9216:10240].bitcast(I32)
    POSF = R2[:, 10240:11264]
    P.dma("sp", [(POSI, pos_in)], writes=["posi"])
    P.op("dve", lambda e, POSF=POSF, POSI=POSI: e.tensor_copy(out=POSF, in_=POSI), reads=["posi"], writes=["posf"])
    for k in range(2):
        for which in range(2):
            dst = TAB[:, 2 * k + which, :]
            if which == 0:
                ts(P, "dve", ANG, POSF, INVF[:, k:k + 1], math.pi / 2, ALU.mult, ALU.add, reads=["posf", "invf"], writes=["ang"])
            else:
                ts(P, "dve", ANG, POSF, INVF[:, k:k + 1], None, ALU.mult, ALU.bypass, reads=["posf", "invf"], writes=["ang"])
            ts(P, "dve", KF, ANG, 1.0 / (2 * math.pi), None, ALU.mult, ALU.bypass, reads=["ang"], writes=["kf"])
            P.op("dve", lambda e, KI=KI, KF=KF: e.tensor_copy(out=KI, in_=KF), reads=["kf"], writes=["ki"])
            P.op("dve", lambda e, KI=KI, KF=KF: e.tensor_copy(out=KF, in_=KI), reads=["ki"], writes=["kf"])
            stt(P, RED, KF, -TWO_PI_HI, ANG, ALU.mult, ALU.add, reads=["kf", "ang"], writes=["red"])
            stt(P, RED, KF, -TWO_PI_LO, RED, ALU.mult, ALU.add, reads=["kf", "red"], writes=["red"])
            ts(P, "dve", MSK, RED, math.pi, -2 * math.pi, ALU.is_gt, ALU.mult, reads=["red"], writes=["msk"])
            tt_op(P, "dve", RED, RED, MSK, ALU.add, reads=["red", "msk"], writes=["red"])
            ts(P, "dve", MSK, RED, -math.pi, 2 * math.pi, ALU.is_lt, ALU.mult, reads=["red"], writes=["msk"])
            tt_op(P, "dve", RED, RED, MSK, ALU.add, reads=["red", "msk"], writes=["red"])
            ts(P, "dve", RED, RED, math.pi, -math.pi, ALU.min, ALU.max, reads=["red"], writes=["red"])
            act(P, dst, RED, AF.Sin, reads=["red"], writes=["tab"])


    for l in range(L):
        def resid_add(tt, grow, l=l):
            for c in range(NCH):
                t = TMP[c % 2]
                stt(P, t[:, :], F32T[:, c, :], GV[:, l, grow, c:c + 1], RSTD[:, :], ALU.mult, ALU.mult,
                    reads=[("f32t", c), "rstd", "gvec"], writes=[("tmp", c % 2)])
                tt_op(P, "dve", Hc(c, tt), Hc(c, tt), t[:, :], ALU.add, reads=[("tmp", c % 2), ("H", c, tt)],
                      writes=[("H", c, tt)])

        def prenorm_to_actb(tt, grow, l=l):
            rms_rstd(P, C, [(Hc(c, tt), [("H", c, tt)]) for c in range(NCH)], TT, D, RSTD[:, :], "rstd")
            for c in range(NCH):
                stt(P, ACTB[:, c, :], Hc(c, tt), GV[:, l, grow, c:c + 1], RSTD[:, :], ALU.mult, ALU.mult,
                    reads=[("H", c, tt), "rstd", "gvec"], writes=[("actb", c)])

        P.barrier()
        CQ = R1[:, 0:2048].rearrange("p (c t) -> p c t", c=4)
        CKV = R1[:, 2048:3072].rearrange("p (c t) -> p c t", c=2)
        CQN = R1[:, 3072:4096].bitcast(BF16).rearrange("p (c t) -> p c t", c=4)
        CKVN = R1[:, 4096:4608].bitcast(BF16).rearrange("p (c t) -> p c t", c=2)
        T1 = [R1[:, 4608 + i * 512:5120 + i * 512] for i in range(2)]
        T2 = [R1[:, 5632 + i * 512:6144 + i * 512] for i in range(2)]
        XB = [R1[:, 6656 + i * 256:6912 + i * 256].bitcast(BF16) for i in range(2)]
        OUTB = [R1[:, 7168 + i * 256:7424 + i * 256].bitcast(BF16) for i in range(4)]
        rope_i = [0]
        outb_i = [0]

        def store_bf16(src_ps_bank, dst_ap, dres, n=TT, parts=128):
            i = outb_i[0]
            outb_i[0] = (i + 1) % 4
            ob = OUTB[i]
            act(P, ob[:parts, :n], P.ps[src_ps_bank][:parts, :n], AF.Copy, reads=[("ps", src_ps_bank)], writes=[("outb", i)])
            P.dma("sp", [(dst_ap, ob[:parts, :n])], reads=[("outb", i)], writes=[dres])

        def rope_store(bank, k, dst_ap, dres, t0):
            i = rope_i[0]
            rope_i[0] = (i + 1) % 2
            xb, t1, t2 = XB[i], T1[i], T2[i]
            act(P, xb, P.ps[bank][:, :], AF.Copy, reads=[("ps", bank)], writes=[("xb", i)])
            tt_op(P, "dve", t1, P.ps[bank][:, :], TAB[:, 2 * k, t0:t0 + TT], ALU.mult, reads=[("ps", bank), "tab"], writes=[("t1", i)])

            def stage_b():
                b2 = P.bank("aux")
                mm_group(P, P.ps[b2][:, :], [(PM[:, k, :], xb)], reads=[("xb", i), "pm"], bank=b2)
                tt_op(P, "dve", t2, P.ps[b2][:, :], TAB[:, 2 * k + 1, t0:t0 + TT], ALU.mult, reads=[("ps", b2), "tab"], writes=[("t2", i)])
                j = outb_i[0]
                outb_i[0] = (j + 1) % 4
                ob = OUTB[j]
                tt_op(P, "dve", ob, t1, t2, ALU.add, reads=[("t1", i), ("t2", i)], writes=[("outb", j)])
                P.dma("sp", [(dst_ap, ob)], reads=[("outb", j)], writes=[dres])

            return stage_b

        for tt in range(2):
            t0 = tt * TT
            prenorm_to_actb(tt, 4)

            def evac_in(oc, bank, s, res, wv, t0=t0, tt=tt):
                if oc < 4:
                    act(P, CQ[:, oc, :], P.ps[bank][:, :], AF.Copy, reads=[("ps", bank)], writes=[("cq", oc)])
                elif oc < 6:
                    act(P, CKV[:, oc - 4, :], P.ps[bank][:, :], AF.Copy, reads=[("ps", bank)], writes=[("ckv", oc - 4)])
                elif oc == 6:
                    return rope_store(bank, 0, KVL["kr"][:, t0:t0 + TT], ("kvloc", "kr", tt), t0)
                elif oc < 15:
                    return rope_store(bank, 1, q_dq[oc - 7][:, t0:t0 + TT], ("qloc", tt), t0)
                else:
                    r0 = (oc - 15) * 128
                    return rope_store(bank, 1, KVL["dk"][r0:r0 + 128, t0:t0 + TT], ("kvloc", "dk", tt), t0)

            linear([w_in[l, oc] for oc in range(23)], NCH, lambda kc: ACTB[:, kc, :], actb_all, evac_in)
            for nh in range(2):
                s, res = wload(w_dv[l, nh], NCH * 512, 3)
                wv = wslot(s, 3)
                for tb in range(4):
                    bank = P.bank("main")
                    mm_group(P, P.ps[bank][:, :],
                             [(ACTB[:, kc, tb * 128:(tb + 1) * 128], wv[:, kc * 512:(kc + 1) * 512]) for kc in range(NCH)],
                             reads=res + actb_all, bank=bank)
                    r0 = t0 + tb * 128
                    store_bf16(bank, KVL["vd"][r0:r0 + 128, nh * 512:(nh + 1) * 512], ("kvloc", "vd", tt))
            rms_rstd(P, C, [(CQ[:, c, :], [("cq", c)]) for c in range(4)], TT, 512, RSTD[:, :], "rstd")
            for c in range(4):
                stt(P, CQN[:, c, :], CQ[:, c, :], GV[:, l, 5, c:c + 1], RSTD[:, :], ALU.mult, ALU.mult,
                    reads=[("cq", c), "rstd", "gvec"], writes=[("cqn", c)])

            def evac_q(oc, bank, s, res, wv, t0=t0, tt=tt):
                if oc < 8:
                    store_bf16(bank, q_qn[oc][:, t0:t0 + TT], ("qloc", tt))
                else:
                    return rope_store(bank, 0, q_qr[oc - 8][:, t0:t0 + TT], ("qloc", tt), t0)

            linear([w_uq[l, oc] for oc in range(12)], 4, lambda kc: CQN[:, kc, :], [("cqn", c) for c in range(4)], evac_q)
            rms_rstd(P, C, [(CKV[:, c, :], [("ckv", c)]) for c in range(2)], TT, 256, RSTD[:, :], "rstd")
            for c in range(2):
                stt(P, CKVN[:, c, :], CKV[:, c, :], GV[:, l, 6, c:c + 1], RSTD[:, :], ALU.mult, ALU.mult,
                    reads=[("ckv", c), "rstd", "gvec"], writes=[("ckvn", c)])

            def evac_kn(oc, bank, s, res, wv, t0=t0, tt=tt):
                store_bf16(bank, KVL["kn"][oc * 128:(oc + 1) * 128, t0:t0 + TT], ("kvloc", "kn", tt))

            ckvn_all = [("ckvn", c) for c in range(2)]
            linear([w_kn[l, oc] for oc in range(8)], 2, lambda kc: CKVN[:, kc, :], ckvn_all, evac_kn)
            s, res = wload(w_vv[l], 2 * 1024)
            wv = wslot(s)
            for tb in range(4):
                for nh in range(2):
                    bank = P.bank("main")
                    mm_group(P, P.ps[bank][:, :],
                             [(CKVN[:, kc, tb * 128:(tb + 1) * 128], wv[:, kc * 1024 + nh * 512:kc * 1024 + (nh + 1) * 512])
                              for kc in range(2)], reads=res + ckvn_all, bank=bank)
                    r0 = t0 + tb * 128
                    store_bf16(bank, KVL["vm"][r0:r0 + 128, nh * 512:(nh + 1) * 512], ("kvloc", "vm", tt))
        for n, r in KV_GROUPS:
            P.collective(KVLT[n], KVAT[n], reads=[("kvloc", n, 0), ("kvloc", n, 1)], writes=[("kvall", n)])

        P.barrier()
        Kb = [R2[:, i * 1024:(i + 1) * 1024].bitcast(BF16) for i in range(2)]
        Vb = [R2[:, 2048 + i * 1024:3072 + i * 1024].bitcast(BF16).rearrange("p (k d) -> p k d", k=16) for i in range(2)]
        Qb = [R2[:, 4096 + i * 512:4608 + i * 512].bitcast(BF16) for i in range(2)]
        QRb = [R2[:, 5120 + i * 512:5632 + i * 512].bitcast(BF16) for i in range(2)]
        KRb = R2[:, 6144:7168].bitcast(BF16)
        NPT = 4
        PTb = [R2[:, 7168 + i * 256:7424 + i * 256].bitcast(BF16) for i in range(NPT)]
        REC = R2[:, 8192:8704]
        A1 = R2[:, 8704:9216]
        A2 = R2[:, 9216:9728]
        ODb = [R2[:, 9728:10240], R2[:, 10752:11264]]
        AOB = [R2[:, 10240 + i * 256:10496 + i * 256].bitcast(BF16) for i in range(2)]
        P.dma("sp", [(LAMV[:, :, :], lam_in[:, l])], writes=["lamv"])
        for j in range(2):
            tt_op(P, "dve", LT[:, :], LAMV[:, 2 * j, :], LAMV[:, 2 * j + 1, :], ALU.mult, reads=["lamv"], writes=["lt"])
            P.op("dve", lambda e, j=j: e.reduce_sum(out=LS[:, j:j + 1], in_=LT[:, :], axis=mybir.AxisListType.X), reads=["lt"], writes=["ls"])
        act(P, LS[:, 2:4], LS[:, 0:2], AF.Exp, reads=["ls"], writes=["ls2"])
        tt_op(P, "dve", NEGLAM[:, :], LS[:, 3:4], LS[:, 2:3], ALU.subtract, reads=["ls2"], writes=["neglam"])
        tt_op(P, "dve", NEGLAM[:, :], NEGLAM[:, :], MISC[:, l, 1:2], ALU.subtract, reads=["neglam", "misc"], writes=["neglam"])
        tt_op(P, "dve", GS[:, :], MISC[:, l, 0:1], MISC[:, l, 2:3], ALU.mult, reads=["misc"], writes=["gs"])
        P.dma("sp", [(KRb[:, TPC:2 * TPC], KVL["kr"][:, :])], reads=[("kvloc", "kr", 0), ("kvloc", "kr", 1)], writes=["krB"])
        P.dma("sp", [(KRb[:, 0:TPC], KVA["kr"][0:128, :])], reads=[("kvall", "kr")], writes=["krA"])
        P.rot = {"main": [0, 1, 2], "aux": [3, 4], "stat": [5, 6, 7]}
        P.roti = {k: 0 for k in P.rot}
        for i in range(2):
            P.op("dve", lambda e, i=i: e.memset(QRb[i][64:128, :], 0.0), writes=[("QR", i)])
        pt_i = [0]
        ob_i = [0]

        def attend(qt, s_pieces_fn, s_reads, vbuf, v_res, scale):
            bo = P.bank("aux")
            bd = P.bank("stat")
            nkb = 8 + 4 * qt + 4
            LOOK = 2
            st = {}

            def issue_s(kb):
                kl = kb - 8
                c0 = max(0, 128 * (kl - 4 * qt)) if kl >= 0 else 0
                sb_ = P.bank("main")
                mm_group(P, P.ps[sb_][:, c0:TT], s_pieces_fn(kb, qt * TT + c0, TT - c0), reads=s_reads(kl >= 0), bank=sb_)
                i = pt_i[0]
                pt_i[0] = (i + 1) % NPT
                pt = PTb[i]
                if kl < 0:
                    act(P, pt[:, c0:TT], P.ps[sb_][:, c0:TT], AF.Exp, reads=[("ps", sb_), "cmask"], writes=[("pt", i)],
                        scale=scale, bias=CMASK[:, 0:1])
                else:
                    act(P, pt[:, c0:TT], P.ps[sb_][:, c0:TT], AF.Exp, reads=[("ps", sb_)], writes=[("pt", i)], scale=scale)
                    if kl >= 4 * qt:
                        P.op("dve", lambda e, pt=pt, c0=c0: e.memset(pt[64:128, c0:c0 + 64], 0.0), reads=[], writes=[("pt", i)])
                st[kb] = (i, pt, c0)

            def issue_pv(kb, first, last):
                i, pt, c0 = st.pop(kb)
                P.op("pe", lambda e, pt=pt, c0=c0, kb=kb, first=first, last=last, bo=bo: e.matmul(
                    P.ps[bo][:, c0:TT], vbuf[:, kb, :], pt[:, c0:TT], start=first, stop=last),
                    reads=[("pt", i)] + v_res(kb >= 8), writes=[("ps", bo)])
                P.op("pe", lambda e, pt=pt, c0=c0, first=first, last=last, bd=bd: e.matmul(
                    P.ps[bd][:, c0:TT], C.ONES[:, :], pt[:, c0:TT], start=first, stop=last),
                    reads=[("pt", i), "ones"], writes=[("ps", bd)])

            order = list(range(8, nkb)) + list(range(8))
            for idx in range(nkb + LOOK):
                if idx < nkb:
                    issue_s(order[idx])
                if idx - LOOK >= 0:
                    issue_pv(order[idx - LOOK], idx - LOOK == 0, idx - LOOK == nkb - 1)
            return bo, bd

        def astore(dst_ap, writer, tt):
            i = ob_i[0]
            ob_i[0] = (i + 1) % 2
            writer(AOB[i], ("aob", i))
            P.dma("sp", [(dst_ap, AOB[i])], reads=[("aob", i)], writes=[("otloc", tt)])

        def load_kv(i, kn_, vn_, h):
            krow = h * 128
            own = [(Kb[i][:, TPC:2 * TPC], KVL[kn_][krow:krow + 128, :]),
                   (Vb[i][:, 8:16, :], KVL[vn_][0:TPC, h * 128:(h + 1) * 128].rearrange("(kb p) d -> p kb d", p=128))]
            prev = [(Kb[i][:, 0:TPC], KVA[kn_][krow:krow + 128, :]),
                    (Vb[i][:, 0:8, :], KVA[vn_][0:TPC, h * 128:(h + 1) * 128].rearrange("(kb p) d -> p kb d", p=128))]
            P.dma("sp", own, reads=[("kvloc", n, t) for n in (kn_, vn_) for t in range(2)], writes=[("KB", i), ("VB", i)])
            P.dma("sp", prev, reads=[("kvall", kn_), ("kvall", vn_)], writes=[("KA", i), ("VA", i)])

        def load_mla(h):
            i = h % 2
            P.dma("sp", [(Qb[i], q_qn[h]), (QRb[i][0:64, :], q_qr[h // 2][(h % 2) * 64:(h % 2) * 64 + 64, :])],
                  reads=[("qloc", 0), ("qloc", 1)], writes=[("Q", i), ("QR", i)])
            load_kv(i, "kn", "vm", h)

        def kres(i, extra):
            return lambda own: [("Q", i), ("QR", i), ("KB" if own else "KA", i)] + [e + ("B" if own else "A") for e in extra]

        def vres(i):
            return lambda own: [("VB" if own else "VA", i)]

        load_mla(0)
        for h in range(8):
            i = h % 2
            if h + 1 < 8:
                load_mla(h + 1)
            for qt in range(2):
                def pieces(kb, q0, n, i=i):
                    return [(Kb[i][:, kb * 128:(kb + 1) * 128], Qb[i][:, q0:q0 + n]),
                            (KRb[:, kb * 128:(kb + 1) * 128], QRb[i][:, q0:q0 + n])]
                bo, bd = attend(qt, pieces, kres(i, ["kr"]), Vb[i], vres(i), MLA_SCALE)
                P.op("dve", lambda e, bd=bd, REC=REC: e.reciprocal(out=REC, in_=P.ps[bd][:, :]), reads=[("ps", bd)], writes=["rec"])
                astore(ot_loc[h][:, qt * TT:(qt + 1) * TT],
                       lambda ob, r, bo=bo: tt_op(P, "dve", ob, P.ps[bo][:, :], REC, ALU.mult, reads=[("ps", bo), "rec"], writes=[r]), qt)
        for i in range(2):
            P.op("dve", lambda e, i=i: e.memset(Qb[i][64:128, :], 0.0), writes=[("Q", i)])
            P.op("dve", lambda e, i=i: e.memset(QRb[i][0:64, :], 0.0), writes=[("QR", i)])
        def load_diff(h):
            i = h % 2
            P.dma("sp", [(Qb[i][0:64, :], q_dq[h][0:64, :]), (QRb[i][64:128, :], q_dq[h][64:128, :])],
                  reads=[("qloc", 0), ("qloc", 1)], writes=[("Q", i), ("QR", i)])
            load_kv(i, "dk", "vd", h)

        load_diff(0)
        pending_fin = [None]
        for h in range(8):
            i = h % 2
            if h + 1 < 8:
                load_diff(h + 1)
            for qt in range(2):
                for m in range(2):
                    def pieces(kb, q0, n, i=i, m=m):
                        qm = Qb[i] if m == 0 else QRb[i]
                        return [(Kb[i][:, kb * 128:(kb + 1) * 128], qm[:, q0:q0 + n])]
                    bo, bd = attend(qt, pieces, kres(i, []), Vb[i], vres(i), DIFF_SCALE)
                    if m == 0 and pending_fin[0] is not None:
                        pending_fin[0]()
                        pending_fin[0] = None
                    P.op("dve", lambda e, bd=bd, REC=REC: e.reciprocal(out=REC, in_=P.ps[bd][:, :]), reads=[("ps", bd)], writes=["rec"])
                    dst = A1 if m == 0 else A2
                    tt_op(P, "dve", dst, P.ps[bo][:, :], REC, ALU.mult, reads=[("ps", bo), "rec"], writes=["a%d" % m])
                od = ODb[(2 * h + qt) % 2]
                odr = ("od", (2 * h + qt) % 2)
                stt(P, od, A2, NEGLAM[:, 0:1], A1, ALU.mult, ALU.add, reads=["a0", "a1", "neglam"], writes=[odr])

                def fin(od=od, odr=odr, h=h, qt=qt):
                    rms_rstd(P, C, [(od, [odr])], TT, 128, RSTD[:, :], "rstd")
                    astore(ot_loc[8 + h][:, qt * TT:(qt + 1) * TT],
                           lambda ob, r: stt(P, ob, od, GS[:, 0:1], RSTD[:, :], ALU.mult, ALU.mult, reads=[odr, "gs", "rstd"], writes=[r]), qt)

                pending_fin[0] = fin
        pending_fin[0]()

        P.barrier()
        P.rot = {"main": [0, 1, 2, 3], "aux": [4, 5], "stat": [6, 7]}
        P.roti = {k: 0 for k in P.rot}
        ABUF = R2[:, :].bitcast(BF16).rearrange("p (c t) -> p c t", c=NFF)
        USB = [[R1[:, (2 * i + j) * 520:(2 * i + j) * 520 + TT + 2] for j in range(2)] for i in range(2)]
        YC = [R1[:, 2080 + j * 512:2592 + j * 512] for j in range(2)]
        GEL = R1[:, 3104:3616]
        PT = R2[:, 0:512].bitcast(BF16).rearrange("p (c t) -> p c t", c=2)
        P.dma("sp", [(CW[:, :, :], cw_in[l]), (CB[:, :], cb_in[l])], writes=["cw"])
        def halo_exchange(l=l):
            rms_rstd(P, C, [(H[:, c, TPC - 2:TPC], [("H", c, 1)]) for c in range(NCH)], 2, D, RSTDH[:, :], "rstdh")
            for c in range(NCH):
                stt(P, HNL[:, c, :], H[:, c, TPC - 2:TPC], GV[:, l, 1, c:c + 1], RSTDH[:, :], ALU.mult, ALU.mult,
                    reads=[("H", c, 1), "rstdh", "gvec"], writes=["hnl"])
            P.dma("sp", [(hl_loc, HNL[:, :, :].rearrange("p c t -> p (c t)"))], reads=["hnl"], writes=["hlloc"])
            P.collective(hl_loc_t, hl_all_t, reads=["hlloc"], writes=["hlall"])
            P.dma("sp", [(HNR[:, :, :].rearrange("p c t -> p (c t)"), hl_all[0:128, :])], reads=["hlall"], writes=["hnr"])
            ts(P, "dve", HNH[0][:, :, :], HNR[:, :, :], CMASK[:, 1:2], None, ALU.mult, ALU.bypass, reads=["hnr", "cmask"], writes=[("hnh", 0)])

        for tt in (1, 0):
            t0 = tt * TT
            P.dma("sp", [(ACTB[:, :, :], ot_loc[:, :, t0:t0 + TT].rearrange("c p t -> p c t"))], reads=[("otloc", tt)], writes=actb_all)

            def evac_wo(oc, bank, s, res, wv):
                act(P, F32T[:, oc, :], P.ps[bank][:, :], AF.Copy, reads=[("ps", bank)], writes=[("f32t", oc)])

            linear([w_o[l, oc] for oc in range(NCH)], NCH, lambda kc: ACTB[:, kc, :], actb_all, evac_wo)
            rms_rstd(P, C, [(F32T[:, c, :], [("f32t", c)]) for c in range(NCH)], TT, D, RSTD[:, :], "rstd")
            resid_add(tt, 0)
            if tt == 1:
                halo_exchange()
        for tt in range(2):
            t0 = tt * TT
            prenorm_to_actb(tt, 1)
            if tt == 0:
                P.op("dve", lambda e: e.tensor_copy(out=HNH[1][:, :, :], in_=ACTB[:, :, TT - 2:TT]),
                     reads=actb_all, writes=[("hnh", 1)])
            hnh = HNH[tt]
            for c in range(NFF):
                par = c % 2
                for j, tile_idx in enumerate((c, NFF + c)):
                    s, res = wload(w_up[l, tile_idx], NCH * 128)
                    wv = wslot(s)
                    bank = P.bank("main")
                    mm_group(P, P.ps[bank][:, :], [(wv[:, kc * 128:(kc + 1) * 128], ACTB[:, kc, :]) for kc in range(NCH)],
                             reads=res + actb_all, bank=bank)
                    hb = P.bank("aux")
                    mm_group(P, P.ps[hb][:, 0:2], [(wv[:, kc * 128:(kc + 1) * 128], hnh[:, kc, :]) for kc in range(NCH)],
                             reads=res + [("hnh", tt)], bank=hb)
                    u = USB[par][j]
                    ur = ("usb", par, j)
                    act(P, u[:, 2:TT + 2], P.ps[bank][:, :], AF.Copy, reads=[("ps", bank)], writes=[ur])
                    act(P, u[:, 0:2], P.ps[hb][:, 0:2], AF.Copy, reads=[("ps", hb)], writes=[ur])
                    y = YC[j]
                    yr = ("yc", j)
                    ts(P, "dve", y, u[:, 2:TT + 2], CW[:, tile_idx, 2:3], CB[:, tile_idx:tile_idx + 1], ALU.mult, ALU.add,
                       reads=[ur, "cw"], writes=[yr])
                    stt(P, y, u[:, 1:TT + 1], CW[:, tile_idx, 1:2], y, ALU.mult, ALU.add, reads=[ur, "cw", yr], writes=[yr])
                    stt(P, y, u[:, 0:TT], CW[:, tile_idx, 0:1], y, ALU.mult, ALU.add, reads=[ur, "cw", yr], writes=[yr])
                act(P, GEL, YC[0], AF.Gelu_apprx_tanh, reads=[("yc", 0)], writes=["gel"])
                tt_op(P, "dve", ABUF[:, c, :], GEL, YC[1], ALU.mult, reads=["gel", ("yc", 1)], writes=[("abuf", c)])

            def evac_dn(oc, bank, s, res, wv):
                act(P, F32T[:, oc, :], P.ps[bank][:, :], AF.Copy, reads=[("ps", bank)], writes=[("f32t", oc)])

            linear([w_dn[l, oc] for oc in range(NCH)], NFF, lambda kc: ABUF[:, kc, :], abuf_all, evac_dn)
            rms_rstd(P, C, [(F32T[:, c, :], [("f32t", c)]) for c in range(NCH)], TT, D, RSTD[:, :], "rstd")
            resid_add(tt, 2)
            for c in range(NCH):
                act(P, ACTB[:, c, :], Hc(c, tt), AF.Copy, reads=[("H", c, tt)], writes=[("actb", c)])
            P.dma("pool", [(PT, pT_in[l, :, :, t0:t0 + TT])], reads=abuf_all, writes=["pt_ple"] + abuf_all)
            for oc in range(NCH):
                s, res = wload(w_pg[l, oc], NCH * 128)
                wv = wslot(s)
                b1 = P.bank("main")
                mm_group(P, P.ps[b1][:, :], [(wv[:, kc * 128:(kc + 1) * 128], ACTB[:, kc, :]) for kc in range(NCH)],
                         reads=res + actb_all, bank=b1)
                s2, res2 = wload(w_pl[l, oc], 2 * 128)
                wv2 = wslot(s2)
                b2 = P.bank("main")
                mm_group(P, P.ps[b2][:, :], [(wv2[:, kc * 128:(kc + 1) * 128], PT[:, kc, :]) for kc in range(2)],
                         reads=res2 + ["pt_ple"], bank=b2)
                act(P, SG[:, :], P.ps[b1][:, :], AF.Sigmoid, reads=[("ps", b1)], writes=["sg"])
                tt_op(P, "dve", F32T[:, oc, :], SG[:, :], P.ps[b2][:, :], ALU.mult, reads=["sg", ("ps", b2)], writes=[("f32t", oc)])
            rms_rstd(P, C, [(F32T[:, c, :], [("f32t", c)]) for c in range(NCH)], TT, D, RSTD[:, :], "rstd")
            resid_add(tt, 3)

    for tt in range(2):
        P.dma("sp", [(hT_out[:, :, tt * TT:(tt + 1) * TT], H[:, :, tt * TT:(tt + 1) * TT])],
              reads=[("H", c, tt) for c in range(NCH)], is_out=True)
    return P


def tile_w(W, cols, KC):
    Wc = W[:, cols]
    M = Wc.shape[1]
    return np.ascontiguousarray(Wc.reshape(KC, 128, M).transpose(1, 0, 2).reshape(128, KC * M))


def tile_w_blocks(W, KC, nblk, bw=128):
    return np.ascontiguousarray(W.reshape(KC, 128, nblk, bw).transpose(2, 1, 0, 3).reshape(nblk, 128, KC * bw))


def fm_vec(g):
    return np.ascontiguousarray(g.reshape(-1, 128).T)


def to_fm(x):
    T, Fd = x.shape
    return np.ascontiguousarray(x.T.reshape(Fd // 128, 128, T).transpose(1, 0, 2))


def from_fm(xf):
    _, Cn, T = xf.shape
    return np.ascontiguousarray(xf.transpose(1, 0, 2).reshape(Cn * 128, T).T)


def const_tables():
    invf = np.zeros((128, 2), np.float32)
    pm = np.zeros((2, 128, 128), np.float32)
    for p in range(128):
        r = p % 64
        invf[p, 0] = THETA ** (-(2.0 * (r % 32)) / 64.0)
        if r < 32:
            pm[0, p + 32, p] = -1.0
        else:
            pm[0, p - 32, p] = 1.0
        if r < 16:
            invf[p, 1] = THETA ** (-(2.0 * (r % 8)) / 16.0)
            if r < 8:
                pm[1, p + 8, p] = -1.0
            else:
                pm[1, p - 8, p] = 1.0
    return invf, pm


def shared_inputs(w, L=DEPTH):
    m = {}
    fm_groups = [list(range(c * 128, (c + 1) * 128)) for c in range(6)]
    fm_groups.append(list(range(768, 832)) * 2)
    fm_groups += [list(range(832 + c * 128, 832 + (c + 1) * 128)) for c in range(16)]
    gq = [list(range(h * 192, h * 192 + 128)) for h in range(8)]
    gq += [list(range((2 * r) * 192 + 128, (2 * r) * 192 + 192)) + list(range((2 * r + 1) * 192 + 128, (2 * r + 1) * 192 + 192))
           for r in range(4)]
    vcols = [h * 256 + 128 + j for h in range(8) for j in range(128)]
    m["w_in_fm"] = np.stack([np.stack([tile_w(w["w_in"][l], g, 16) for g in fm_groups]) for l in range(L)])
    m["w_in_dv"] = np.stack([tile_w_blocks(w["w_in"][l][:, 2880:3904], 16, 2, 512) for l in range(L)])
    m["w_uq"] = np.stack([np.stack([tile_w(w["w_uq"][l], g, 4) for g in gq]) for l in range(L)])
    m["w_ukv_k"] = np.stack([np.stack([tile_w(w["w_ukv"][l], list(range(h * 256, h * 256 + 128)), 2) for h in range(8)]) for l in range(L)])
    m["w_ukv_v"] = np.stack([tile_w(w["w_ukv"][l], vcols, 2) for l in range(L)])
    m["w_o"] = np.stack([tile_w_blocks(w["w_o"][l], 16, 16) for l in range(L)])
    m["w_up"] = np.stack([tile_w_blocks(w["w_up"][l], 16, 2 * NFF) for l in range(L)])
    m["conv_w"] = np.stack([np.ascontiguousarray(w["conv_w"][l].reshape(3, 2 * NFF, 128).transpose(2, 1, 0)) for l in range(L)])
    m["conv_b"] = np.stack([fm_vec(w["conv_b"][l]) for l in range(L)])
    m["w_down"] = np.stack([tile_w_blocks(w["w_down"][l], NFF, 16) for l in range(L)])
    m["w_ple_gate"] = np.stack([tile_w_blocks(w["w_ple_gate"][l], 16, 16) for l in range(L)])
    m["w_ple"] = np.stack([tile_w_blocks(w["w_ple"][l], 2, 16) for l in range(L)])
    g = np.zeros((128, L, 8, NCH), np.float32)
    for l in range(L):
        for r, k in enumerate(("g_mix_post", "g_ffn_pre", "g_ffn_post", "g_ple", "g_mix_pre")):
            g[:, l, r, :] = fm_vec(w[k][l])
        g[:, l, 5, :4] = fm_vec(w["g_q_lora"][l])
        g[:, l, 6, :2] = fm_vec(w["g_kv_lora"][l])
    m["gvec"] = g
    lamv = np.stack([np.stack([w[k][l] for k in ("lambda_q1", "lambda_k1", "lambda_q2", "lambda_k2")]) for l in range(L)])
    m["lamv"] = np.ascontiguousarray(np.broadcast_to(lamv[None], (128, L, 4, 64))).astype(np.float32)
    misc = np.zeros((128, L, 4), np.float32)
    for l in range(L):
        linit = 0.8 - 0.6 * math.exp(-0.3 * l)
        misc[:, l, 0] = w["g_diff_sub"][l]
        misc[:, l, 1] = linit
        misc[:, l, 2] = 1.0 - linit
    m["misc"] = misc
    invf, pm = const_tables()
    m["invf"] = invf
    m["pm"] = pm
    return m


def core_inputs(w, core, L=DEPTH):
    b, half = core // 2, core % 2
    t0 = half * TPC
    m = {}
    m["hT"] = to_fm(w["x"][b, t0:t0 + TPC])
    pos = w["positions"][b, t0:t0 + TPC].astype(np.int32)
    m["pos"] = np.ascontiguousarray(np.broadcast_to(pos[None, :], (128, TPC)))
    m["pT"] = np.stack([to_fm(w["p"][l, b, t0:t0 + TPC]) for l in range(L)])
    cm = np.zeros((128, 2), np.float32)
    cm[:, 0] = 0.0 if half == 1 else -200.0
    cm[:, 1] = 1.0 if half == 1 else 0.0
    m["cmask"] = cm
    return m


_PROG = {}


def get_prog(L=DEPTH):
    if L not in _PROG:
        P = build_fused(L)
        P.finish()
        _PROG[L] = P
    return _PROG[L]


def kernel(**inputs):
    w = {k: np.asarray(v) for k, v in inputs.items()}
    P = get_prog(DEPTH)
    sh = shared_inputs(w)
    maps = []
    for c in range(8):
        m = dict(sh)
        m.update(core_inputs(w, c))
        maps.append(m)
    res = run_bass_kernel_spmd(P.nc, maps, core_ids=list(range(8))).results
    out = np.zeros((B, S, D), np.float32)
    for c in range(8):
        b, half = c // 2, c % 2
        out[b, half * TPC:(half + 1) * TPC] = from_fm(np.asarray(res[c]["hT_out"]))
    return out
```

```python
import math
from contextlib import ExitStack

import numpy as np
import ml_dtypes

import concourse.bass as bass
import concourse.mybir as mybir
from concourse.bass_utils import run_bass_kernel_spmd

F32 = mybir.dt.float32
BF16 = mybir.dt.bfloat16
I32 = mybir.dt.int32
AF = mybir.ActivationFunctionType
ALU = mybir.AluOpType

D = 2048
B = 4
S = 2048
DEPTH = 4
TPC = 1024
TT = 512
NCH = 16
DFF = 5632
NFF = 44
EPS = 1e-6
THETA = 500000.0
TWO_PI_HI = 6.28125
TWO_PI_LO = 2.0 * math.pi - 6.28125
NPBF = ml_dtypes.bfloat16


class Prog:
    ENG = ("pe", "act", "dve", "pool", "sp")

    def __init__(self):
        self.nc = bass.Bass("TRN2", target_bir_lowering=False)
        self.es = ExitStack()
        nc = self.nc
        self.q = {e: [] for e in self.ENG}
        self.cnt = {e: 0 for e in self.ENG}
        self.semobj = {}
        for e in ("pe", "act", "dve", "pool"):
            self.semobj[e] = self.es.enter_context(nc.semaphore(f"s_{e}"))
        self.dring = {}
        self.dtot = {}
        self.dnext = {}
        for qn, n in (("sp", 12), ("pool", 8)):
            keys = []
            for i in range(n):
                k = f"d_{qn}_{i}"
                self.semobj[k] = self.es.enter_context(nc.semaphore(k))
                self.dtot[k] = 0
                keys.append(k)
            self.dring[qn] = keys
            self.dnext[qn] = 0
        self.semobj["cc"] = self.es.enter_context(nc.semaphore("s_cc"))
        self.ccn = 0
        self.pending = {e: {} for e in self.ENG}
        self.known = {e: {} for e in self.ENG}
        self.lastw = {}
        self.readers = {}
        self.out_tokens = {}
        self.ps = [self.es.enter_context(nc.psum_tensor(f"ps{i}", [128, 512], F32)) for i in range(8)]
        self.rot = {"main": [0, 1, 2, 3], "aux": [4, 5], "stat": [6, 7]}
        self.roti = {k: 0 for k in self.rot}
        self.dram_in = {}
        self.dram_out = {}

    def inp(self, name, shape, dt=F32):
        t = self.nc.dram_tensor(name, list(shape), dt, kind="ExternalInput").ap()
        self.dram_in[name] = t
        return t

    def outp(self, name, shape, dt=F32):
        t = self.nc.dram_tensor(name, list(shape), dt, kind="ExternalOutput").ap()
        self.dram_out[name] = t
        return t

    def sb(self, name, shape, dt):
        return self.es.enter_context(self.nc.sbuf_tensor(name, list(shape), dt))

    def bank(self, kind="main"):
        r = self.rot[kind]
        i = self.roti[kind]
        self.roti[kind] = (i + 1) % len(r)
        return r[i]

    def _deps(self, eng, reads, writes, extra=()):
        deps = {}

        def add(tok):
            if tok is None:
                return
            k, v = tok
            if deps.get(k, 0) < v:
                deps[k] = v

        for r in reads:
            add(self.lastw.get(r))
            if isinstance(r, tuple) and r[0] == "ps":
                for k, v in self.readers.get(r, {}).items():
                    if k != eng:
                        add((k, v))
        for w in writes:
            add(self.lastw.get(w))
            for k, v in self.readers.get(w, {}).items():
                add((k, v))
        for t in extra:
            add(t)
        for k, v in self.pending[eng].items():
            add((k, v))
        self.pending[eng] = {}
        waits = []
        for k, v in deps.items():
            if eng == "pe" and k == "pe":
                continue
            if self.known[eng].get(k, 0) >= v:
                continue
            self.known[eng][k] = v
            waits.append((k, v))
        return waits

    def _commit(self, tok, reads, writes):
        for w in writes:
            self.lastw[w] = tok
            self.readers[w] = {}
        for r in reads:
            d = self.readers.setdefault(r, {})
            if d.get(tok[0], 0) < tok[1]:
                d[tok[0]] = tok[1]

    def op(self, eng, fn, reads=(), writes=()):
        waits = self._deps(eng, reads, writes)
        self.cnt[eng] += 1
        tok = (eng, self.cnt[eng])
        self.q[eng].append((waits, fn, True))
        self._commit(tok, reads, writes)
        return tok

    def dma(self, qn, pairs, reads=(), writes=(), is_out=False):
        ring = self.dring[qn]
        i = self.dnext[qn]
        self.dnext[qn] = (i + 1) % len(ring)
        k = ring[i]
        prev = self.dtot[k]
        extra = [(k, prev)] if prev > 0 else []
        waits = self._deps(qn, reads, writes, extra)
        total = prev + 16 * len(pairs)
        self.dtot[k] = total
        sem = self.semobj[k]

        def fn(e, pairs=pairs, sem=sem):
            for (o, a) in pairs:
                e.dma_start(out=o, in_=a).then_inc(sem, 16)
            return None

        self.q[qn].append((waits, fn, False))
        tok = (k, total)
        self._commit(tok, reads, writes)
        if is_out:
            self.out_tokens[k] = max(self.out_tokens.get(k, 0), total)
        return tok

    def barrier(self):
        toks = {e: self.cnt[e] for e in ("pe", "act", "dve", "pool") if self.cnt[e] > 0}
        for k, v in self.dtot.items():
            if v > 0:
                toks[k] = v
        for e in self.ENG:
            for k, v in toks.items():
                if self.pending[e].get(k, 0) < v:
                    self.pending[e][k] = v

    def collective(self, t_in, t_out, reads=(), writes=()):
        waits = self._deps("pool", reads, writes)
        self.ccn += 1
        sem = self.semobj["cc"]

        def fn(e, t_in=t_in, t_out=t_out, sem=sem):
            e.collective_compute("AllGather", ALU.bypass, replica_groups=RG, ins=[t_in.ap().opt()],
                                 outs=[t_out.ap().opt()]).then_inc(sem)
            return None

        self.q["pool"].append((waits, fn, False))
        tok = ("cc", self.ccn)
        self._commit(tok, reads, writes)
        return tok

    def finish(self):
        fw = [(k, v) for k, v in self.out_tokens.items()]
        self.q["sp"].append((fw, None, False))
        nc = self.nc
        emap = {"pe": "tensor", "act": "scalar", "dve": "vector", "pool": "gpsimd", "sp": "sync"}

        def emit(ename, e):
            for waits, fn, inc in self.q[ename]:
                for (k, v) in waits:
                    e.wait_ge(self.semobj[k], v)
                if fn is None:
                    continue
                inst = fn(e)
                if inc:
                    inst.then_inc(self.semobj[ename], 1)

        with nc.Block() as block:
            @block.tensor
            def _(e):
                emit("pe", e)

            @block.scalar
            def _(e):
                emit("act", e)

            @block.vector
            def _(e):
                emit("dve", e)

            @block.gpsimd
            def _(e):
                emit("pool", e)

            @block.sync
            def _(e):
                emit("sp", e)
        self.es.close()
        return nc


class Ctx:
    pass


def mm_group(P, bank_ap, pieces, reads, bank, extra_writes=()):
    n = len(pieces)

    def fn(e, pieces=pieces, bank_ap=bank_ap):
        inst = None
        for i, (l, r) in enumerate(pieces):
            inst = e.matmul(bank_ap, l, r, start=(i == 0), stop=(i == n - 1))
        return inst

    return P.op("pe", fn, reads=reads, writes=[("ps", bank)] + list(extra_writes))


def act(P, out, in_, func, reads, writes, **kw):
    return P.op("act", lambda e: e.activation(out=out, in_=in_, func=func, **kw), reads=reads, writes=writes)


def tt_op(P, eng, out, in0, in1, op, reads, writes):
    return P.op(eng, lambda e: e.tensor_tensor(out=out, in0=in0, in1=in1, op=op), reads=reads, writes=writes)


def stt(P, out, in0, scalar, in1, op0, op1, reads, writes):
    return P.op("dve", lambda e: e.scalar_tensor_tensor(out=out, in0=in0, scalar=scalar, in1=in1, op0=op0, op1=op1),
                reads=reads, writes=writes)


def ts(P, eng, out, in0, s1, s2, op0, op1, reads, writes):
    if s2 is None:
        return P.op(eng, lambda e: e.tensor_scalar(out=out, in0=in0, scalar1=s1, scalar2=None, op0=op0),
                    reads=reads, writes=writes)
    return P.op(eng, lambda e: e.tensor_scalar(out=out, in0=in0, scalar1=s1, scalar2=s2, op0=op0, op1=op1),
                reads=reads, writes=writes)


def rms_rstd(P, C, chunks, n, dim, rstd_ap, rstd_res):
    bank = P.bank("stat")
    nchk = len(chunks)
    for i, (ap, rd) in enumerate(chunks):
        j = C.sq_i
        C.sq_i = (C.sq_i + 1) % len(C.SQ)
        sq = C.SQ[j][:, :n]
        act(P, sq, ap, AF.Square, reads=rd, writes=[("sq", j)])
        P.op("pe", lambda e, sq=sq, i=i: e.matmul(P.ps[bank][:, :n], C.ONES[:, :], sq, start=(i == 0), stop=(i == nchk - 1)),
             reads=[("sq", j), "ones"], writes=[("ps", bank)])
    act(P, C.SQT[:, :n], P.ps[bank][:, :n], AF.Sqrt, reads=[("ps", bank), "epsv"], writes=["sqt"], bias=C.EPSV[:, 0:1], scale=1.0 / dim)
    P.op("dve", lambda e: e.reciprocal(out=rstd_ap, in_=C.SQT[:, :n]), reads=["sqt"], writes=[rstd_res])


WSLOT = 2816
NWS = 4
RG = [[0, 1], [2, 3], [4, 5], [6, 7]]
KV_ROWS = 4224
R_KN, R_KR, R_DK, R_VM, R_VD = 0, 1024, 1152, 2176, 3200
MLA_SCALE = 192 ** -0.5
DIFF_SCALE = 64 ** -0.5


def build_fused(L=DEPTH):
    P = Prog()
    nc = P.nc
    C = Ctx()
    C.sq_i = 0
    H = P.sb("H", [128, NCH, TPC], F32)
    R1 = P.sb("R1", [128, 8192], F32)
    R2 = P.sb("R2", [128, 11264], F32)
    ACTB = P.sb("ACTB", [128, NCH, TT], BF16)
    WS = P.sb("WS", [128, NWS, WSLOT], BF16)
    C.ONES = P.sb("ONES", [128, 128], BF16)
    C.SQ = [P.sb(f"SQ{i}", [128, TT], BF16) for i in range(2)]
    C.SQT = P.sb("SQT", [128, TT], F32)
    C.EPSV = P.sb("EPSV", [128, 1], F32)
    RSTD = P.sb("RSTD", [128, TT], F32)
    TMP = [R2[:, 1024 + i * 512:1536 + i * 512] for i in range(2)]
    GV = P.sb("GV", [128, L, 8, NCH], F32)
    CW = P.sb("CW", [128, 2 * NFF, 3], F32)
    CB = P.sb("CB", [128, 2 * NFF], F32)
    HNH = [P.sb(f"HNH{i}", [128, NCH, 2], BF16) for i in range(2)]
    HNL = P.sb("HNL", [128, NCH, 2], BF16)
    HNR = P.sb("HNR", [128, NCH, 2], BF16)
    RSTDH = P.sb("RSTDH", [128, 2], F32)
    UH = P.sb("UH", [128, 2 * NFF, 2], F32)
    SG = R2[:, 512:1024]
    INVF = P.sb("INVF", [128, 2], F32)
    PM = P.sb("PM", [128, 2, 128], BF16)
    CMASK = P.sb("CMASK", [128, 2], F32)
    LAMV = P.sb("LAMV", [128, 4, 64], F32)
    MISC = P.sb("MISC", [128, L, 4], F32)
    LT = P.sb("LT", [128, 64], F32)
    LS = P.sb("LS", [128, 4], F32)
    NEGLAM = P.sb("NEGLAM", [128, 1], F32)
    GS = P.sb("GS", [128, 1], F32)
    F32T = R1[:, :].rearrange("p (c t) -> p c t", c=NCH)
    TAB = P.sb("TAB", [128, 4, TPC], F32)

    hT_in = P.inp("hT", [128, NCH, TPC])
    gv_in = P.inp("gvec", [128, L, 8, NCH])
    pos_in = P.inp("pos", [128, TPC], I32)
    inv_in = P.inp("invf", [128, 2])
    pm_in = P.inp("pm", [2, 128, 128])
    cm_in = P.inp("cmask", [128, 2])
    lam_in = P.inp("lamv", [128, L, 4, 64])
    misc_in = P.inp("misc", [128, L, 4])
    w_in = P.inp("w_in_fm", [L, 23, 128, NCH * 128])
    w_dv = P.inp("w_in_dv", [L, 2, 128, NCH * 512])
    w_uq = P.inp("w_uq", [L, 12, 128, 4 * 128])
    w_kn = P.inp("w_ukv_k", [L, 8, 128, 2 * 128])
    w_vv = P.inp("w_ukv_v", [L, 128, 2 * 1024])
    w_o = P.inp("w_o", [L, NCH, 128, NCH * 128])
    w_up = P.inp("w_up", [L, 2 * NFF, 128, NCH * 128])
    cw_in = P.inp("conv_w", [L, 128, 2 * NFF, 3])
    cb_in = P.inp("conv_b", [L, 128, 2 * NFF])
    w_dn = P.inp("w_down", [L, NCH, 128, NFF * 128])
    w_pg = P.inp("w_ple_gate", [L, NCH, 128, NCH * 128])
    w_pl = P.inp("w_ple", [L, NCH, 128, 2 * 128])
    pT_in = P.inp("pT", [L, 128, 2, TPC])
    hT_out = P.outp("hT_out", [128, NCH, TPC])
    q_qn = nc.dram_tensor("s_qn", [8, 128, TPC], BF16).ap()
    q_qr = nc.dram_tensor("s_qr", [4, 128, TPC], BF16).ap()
    q_dq = nc.dram_tensor("s_dq", [8, 128, TPC], BF16).ap()
    KV_GROUPS = (("kr", 128), ("kn", 1024), ("vm", 1024), ("dk", 1024), ("vd", 1024))
    KVLT = {n: nc.dram_tensor(f"s_{n}_loc", [r, TPC], BF16) for n, r in KV_GROUPS}
    KVAT = {n: nc.dram_tensor(f"s_{n}_all", [2 * r, TPC], BF16) for n, r in KV_GROUPS}
    KVL = {n: t.ap() for n, t in KVLT.items()}
    KVA = {n: t.ap() for n, t in KVAT.items()}
    ot_loc = nc.dram_tensor("s_ot", [NCH, 128, TPC], BF16).ap()
    hl_loc_t = nc.dram_tensor("s_hloc", [128, 2 * NCH], BF16)
    hl_all_t = nc.dram_tensor("s_hall", [256, 2 * NCH], BF16)
    hl_loc, hl_all = hl_loc_t.ap(), hl_all_t.ap()

    ws_i = [0]

    def wload(src_ap, nelem, nslots=1):
        s = ws_i[0]
        if s + nslots > NWS:
            s = 0
        ws_i[0] = (s + nslots) % NWS
        res = [("w", s + k) for k in range(nslots)]
        dst = WS[:, s:s + nslots, :].rearrange("p s e -> p (s e)")[:, :nelem]
        ch = max(d for d in range(1, 2049) if nelem % d == 0)
        if ch != nelem:
            dst = dst.rearrange("p (a b) -> p a b", b=ch)
            src_ap = src_ap.rearrange("p (a b) -> p a b", b=ch)
        P.dma("pool", [(dst, src_ap)], reads=[], writes=res)
        return s, res

    def wslot(s, nslots=1):
        return WS[:, s:s + nslots, :].rearrange("p s e -> p (s e)")

    P.op("pool", lambda e: e.memset(C.ONES[:, :], 1.0), writes=["ones"])
    P.op("pool", lambda e: e.memset(C.EPSV[:, :], EPS), writes=["epsv"])
    P.dma("sp", [(GV[:, :, :, :], gv_in), (INVF[:, :], inv_in), (CMASK[:, :], cm_in),
                 (MISC[:, :, :], misc_in)], writes=["gvec", "invf", "cmask", "misc"])
    P.dma("pool", [(PM[:, :, :], pm_in.rearrange("k p m -> p k m"))], writes=["pm"])
    for tt in range(2):
        P.dma("sp", [(H[:, :, tt * TT:(tt + 1) * TT], hT_in[:, :, tt * TT:(tt + 1) * TT])],
              writes=[("H", c, tt) for c in range(NCH)])

    def Hc(c, tt):
        return H[:, c, tt * TT:(tt + 1) * TT]

    actb_all = [("actb", c) for c in range(NCH)]
    abuf_all = [("abuf", c) for c in range(NFF)]

    def linear(w_tiles, KC, in_fn, in_reads, evac, M=128, n=TT):
        deferred = None
        for oc, w in enumerate(w_tiles):
            nsl = (KC * M + WSLOT - 1) // WSLOT
            s, res = wload(w, KC * M, nsl)
            wv = wslot(s, nsl)
            bank = P.bank("main")
            mm_group(P, P.ps[bank][:M, :n], [(wv[:, kc * M:(kc + 1) * M], in_fn(kc)) for kc in range(KC)],
                     reads=res + in_reads, bank=bank)
            if deferred is not None:
                deferred()
            deferred = evac(oc, bank, s, res, wv)
        if deferred is not None:
            deferred()

    ANG = R2[:, 4096:5120]
    RED = R2[:, 5120:6144]
    KI = R2[:, 6144:7168].bitcast(I32)
    KF = R2[:, 7168:8192]
    MSK = R2[:, 8192:9216]
    POSI = R2[:, 9216:10240].bitcast(I32)
    POSF = R2[:, 10240:11264]
    P.dma("sp", [(POSI, pos_in)], writes=["posi"])
    P.op("dve", lambda e, POSF=POSF, POSI=POSI: e.tensor_copy(out=POSF, in_=POSI), reads=["posi"], writes=["posf"])
    for k in range(2):
        for which in range(2):
            dst = TAB[:, 2 * k + which, :]
            if which == 0:
                ts(P, "dve", ANG, POSF, INVF[:, k:k + 1], math.pi / 2, ALU.mult, ALU.add, reads=["posf", "invf"], writes=["ang"])
            else:
                ts(P, "dve", ANG, POSF, INVF[:, k:k + 1], None, ALU.mult, ALU.bypass, reads=["posf", "invf"], writes=["ang"])
            ts(P, "dve", KF, ANG, 1.0 / (2 * math.pi), None, ALU.mult, ALU.bypass, reads=["ang"], writes=["kf"])
            P.op("dve", lambda e, KI=KI, KF=KF: e.tensor_copy(out=KI, in_=KF), reads=["kf"], writes=["ki"])
            P.op("dve", lambda e, KI=KI, KF=KF: e.tensor_copy(out=KF, in_=KI), reads=["ki"], writes=["kf"])
            stt(P, RED, KF, -TWO_PI_HI, ANG, ALU.mult, ALU.add, reads=["kf", "ang"], writes=["red"])
            stt(P, RED, KF, -TWO_PI_LO, RED, ALU.mult, ALU.add, reads=["kf", "red"], writes=["red"])
            ts(P, "dve", MSK, RED, math.pi, -2 * math.pi, ALU.is_gt, ALU.mult, reads=["red"], writes=["msk"])
            tt_op(P, "dve", RED, RED, MSK, ALU.add, reads=["red", "msk"], writes=["red"])
            ts(P, "dve", MSK, RED, -math.pi, 2 * math.pi, ALU.is_lt, ALU.mult, reads=["red"], writes=["msk"])
            tt_op(P, "dve", RED, RED, MSK, ALU.add, reads=["red", "msk"], writes=["red"])
            ts(P, "dve", RED, RED, math.pi, -math.pi, ALU.min, ALU.max, reads=["red"], writes=["red"])
            act(P, dst, RED, AF.Sin, reads=["red"], writes=["tab"])


    for l in range(L):
        def resid_add(tt, grow, l=l):
            for c in range(NCH):
                t = TMP[c % 2]
                stt(P, t[:, :], F32T[:, c, :], GV[:, l, grow, c:c + 1], RSTD[:, :], ALU.mult, ALU.mult,
                    reads=[("f32t", c), "rstd", "gvec"], writes=[("tmp", c % 2)])
                tt_op(P, "dve", Hc(c, tt), Hc(c, tt), t[:, :], ALU.add, reads=[("tmp", c % 2), ("H", c, tt)],
                      writes=[("H", c, tt)])

        def prenorm_to_actb(tt, grow, l=l):
            rms_rstd(P, C, [(Hc(c, tt), [("H", c, tt)]) for c in range(NCH)], TT, D, RSTD[:, :], "rstd")
            for c in range(NCH):
                stt(P, ACTB[:, c, :], Hc(c, tt), GV[:, l, grow, c:c + 1], RSTD[:, :], ALU.mult, ALU.mult,
                    reads=[("H", c, tt), "rstd", "gvec"], writes=[("actb", c)])

        P.barrier()
        CQ = R1[:, 0:2048].rearrange("p (c t) -> p c t", c=4)
        CKV = R1[:, 2048:3072].rearrange("p (c t) -> p c t", c=2)
        CQN = R1[:, 3072:4096].bitcast(BF16).rearrange("p (c t) -> p c t", c=4)
        CKVN = R1[:, 4096:4608].bitcast(BF16).rearrange("p (c t) -> p c t", c=2)
        T1 = [R1[:, 4608 + i * 512:5120 + i * 512] for i in range(2)]
        T2 = [R1[:, 5632 + i * 512:6144 + i * 512] for i in range(2)]
        XB = [R1[:, 6656 + i * 256:6912 + i * 256].bitcast(BF16) for i in range(2)]
        OUTB = [R1[:, 7168 + i * 256:7424 + i * 256].bitcast(BF16) for i in range(4)]
        rope_i = [0]
        outb_i = [0]

        def store_bf16(src_ps_bank, dst_ap, dres, n=TT, parts=128):
            i = outb_i[0]
            outb_i[0] = (i + 1) % 4
            ob = OUTB[i]
            act(P, ob[:parts, :n], P.ps[src_ps_bank][:parts, :n], AF.Copy, reads=[("ps", src_ps_bank)], writes=[("outb", i)])
            P.dma("sp", [(dst_ap, ob[:parts, :n])], reads=[("outb", i)], writes=[dres])

        def rope_store(bank, k, dst_ap, dres, t0):
            i = rope_i[0]
            rope_i[0] = (i + 1) % 2
            xb, t1, t2 = XB[i], T1[i], T2[i]
            act(P, xb, P.ps[bank][:, :], AF.Copy, reads=[("ps", bank)], writes=[("xb", i)])
            tt_op(P, "dve", t1, P.ps[bank][:, :], TAB[:, 2 * k, t0:t0 + TT], ALU.mult, reads=[("ps", bank), "tab"], writes=[("t1", i)])

            def stage_b():
                b2 = P.bank("aux")
                mm_group(P, P.ps[b2][:, :], [(PM[:, k, :], xb)], reads=[("xb", i), "pm"], bank=b2)
                tt_op(P, "dve", t2, P.ps[b2][:, :], TAB[:, 2 * k + 1, t0:t0 + TT], ALU.mult, reads=[("ps", b2), "tab"], writes=[("t2", i)])
                j = outb_i[0]
                outb_i[0] = (j + 1) % 4
                ob = OUTB[j]
                tt_op(P, "dve", ob, t1, t2, ALU.add, reads=[("t1", i), ("t2", i)], writes=[("outb", j)])
                P.dma("sp", [(dst_ap, ob)], reads=[("outb", j)], writes=[dres])

            return stage_b

        prenorm_to_actb(0, 4)
        for tt in range(2):
            t0 = tt * TT

            def evac_in(oc, bank, s, res, wv, t0=t0, tt=tt):
                if oc < 4:
                    act(P, CQ[:, oc, :], P.ps[bank][:, :], AF.Copy, reads=[("ps", bank)], writes=[("cq", oc)])
                elif oc < 6:
                    act(P, CKV[:, oc - 4, :], P.ps[bank][:, :], AF.Copy, reads=[("ps", bank)], writes=[("ckv", oc - 4)])
                elif oc == 6:
                    return rope_store(bank, 0, KVL["kr"][:, t0:t0 + TT], ("kvloc", "kr", tt), t0)
                elif oc < 15:
                    return rope_store(bank, 1, q_dq[oc - 7][:, t0:t0 + TT], ("qloc", tt), t0)
                else:
                    r0 = (oc - 15) * 128
                    return rope_store(bank, 1, KVL["dk"][r0:r0 + 128, t0:t0 + TT], ("kvloc", "dk", tt), t0)

            linear([w_in[l, oc] for oc in range(23)], NCH, lambda kc: ACTB[:, kc, :], actb_all, evac_in)
            for nh in range(2):
                s, res = wload(w_dv[l, nh], NCH * 512, 3)
                wv = wslot(s, 3)
                for tb in range(4):
                    bank = P.bank("main")
                    mm_group(P, P.ps[bank][:, :],
                             [(ACTB[:, kc, tb * 128:(tb + 1) * 128], wv[:, kc * 512:(kc + 1) * 512]) for kc in range(NCH)],
                             reads=res + actb_all, bank=bank)
                    r0 = t0 + tb * 128
                    store_bf16(bank, KVL["vd"][r0:r0 + 128, nh * 512:(nh + 1) * 512], ("kvloc", "vd", tt))
            if tt == 0:
                prenorm_to_actb(1, 4)
            rms_rstd(P, C, [(CQ[:, c, :], [("cq", c)]) for c in range(4)], TT, 512, RSTD[:, :], "rstd")
            for c in range(4):
                stt(P, CQN[:, c, :], CQ[:, c, :], GV[:, l, 5, c:c + 1], RSTD[:, :], ALU.mult, ALU.mult,
                    reads=[("cq", c), "rstd", "gvec"], writes=[("cqn", c)])

            def evac_q(oc, bank, s, res, wv, t0=t0, tt=tt):
                if oc < 8:
                    store_bf16(bank, q_qn[oc][:, t0:t0 + TT], ("qloc", tt))
                else:
                    return rope_store(bank, 0, q_qr[oc - 8][:, t0:t0 + TT], ("qloc", tt), t0)

            linear([w_uq[l, oc] for oc in range(12)], 4, lambda kc: CQN[:, kc, :], [("cqn", c) for c in range(4)], evac_q)
            rms_rstd(P, C, [(CKV[:, c, :], [("ckv", c)]) for c in range(2)], TT, 256, RSTD[:, :], "rstd")
            for c in range(2):
                stt(P, CKVN[:, c, :], CKV[:, c, :], GV[:, l, 6, c:c + 1], RSTD[:, :], ALU.mult, ALU.mult,
                    reads=[("ckv", c), "rstd", "gvec"], writes=[("ckvn", c)])

            def evac_kn(oc, bank, s, res, wv, t0=t0, tt=tt):
                store_bf16(bank, KVL["kn"][oc * 128:(oc + 1) * 128, t0:t0 + TT], ("kvloc", "kn", tt))

            ckvn_all = [("ckvn", c) for c in range(2)]
            linear([w_kn[l, oc] for oc in range(8)], 2, lambda kc: CKVN[:, kc, :], ckvn_all, evac_kn)
            s, res = wload(w_vv[l], 2 * 1024)
            wv = wslot(s)
            for tb in range(4):
                for nh in range(2):
                    bank = P.bank("main")
                    mm_group(P, P.ps[bank][:, :],
                             [(CKVN[:, kc, tb * 128:(tb + 1) * 128], wv[:, kc * 1024 + nh * 512:kc * 1024 + (nh + 1) * 512])
                              for kc in range(2)], reads=res + ckvn_all, bank=bank)
                    r0 = t0 + tb * 128
                    store_bf16(bank, KVL["vm"][r0:r0 + 128, nh * 512:(nh + 1) * 512], ("kvloc", "vm", tt))
        for n, r in KV_GROUPS:
            P.collective(KVLT[n], KVAT[n], reads=[("kvloc", n, 0), ("kvloc", n, 1)], writes=[("kvall", n)])

        P.barrier()
        Kb = [R2[:, i * 1024:(i + 1) * 1024].bitcast(BF16) for i in range(2)]
        Vb = [R2[:, 2048 + i * 1024:3072 + i * 1024].bitcast(BF16).rearrange("p (k d) -> p k d", k=16) for i in range(2)]
        Qb = [R2[:, 4096 + i * 512:4608 + i * 512].bitcast(BF16) for i in range(2)]
        QRb = [R2[:, 5120 + i * 512:5632 + i * 512].bitcast(BF16) for i in range(2)]
        KRb = R2[:, 6144:7168].bitcast(BF16)
        NPT = 4
        PTb = [R2[:, 7168 + i * 256:7424 + i * 256].bitcast(BF16) for i in range(NPT)]
        REC = R2[:, 8192:8704]
        A1 = R2[:, 8704:9216]
        A2 = R2[:, 9216:9728]
        ODb = [R2[:, 9728:10240], R2[:, 10752:11264]]
        AOB = [R2[:, 10240 + i * 256:10496 + i * 256].bitcast(BF16) for i in range(2)]
        P.dma("sp", [(LAMV[:, :, :], lam_in[:, l])], writes=["lamv"])
        for j in range(2):
            tt_op(P, "dve", LT[:, :], LAMV[:, 2 * j, :], LAMV[:, 2 * j + 1, :], ALU.mult, reads=["lamv"], writes=["lt"])
            P.op("dve", lambda e, j=j: e.reduce_sum(out=LS[:, j:j + 1], in_=LT[:, :], axis=mybir.AxisListType.X), reads=["lt"], writes=["ls"])
        act(P, LS[:, 2:4], LS[:, 0:2], AF.Exp, reads=["ls"], writes=["ls2"])
        tt_op(P, "dve", NEGLAM[:, :], LS[:, 3:4], LS[:, 2:3], ALU.subtract, reads=["ls2"], writes=["neglam"])
        tt_op(P, "dve", NEGLAM[:, :], NEGLAM[:, :], MISC[:, l, 1:2], ALU.subtract, reads=["neglam", "misc"], writes=["neglam"])
        tt_op(P, "dve", GS[:, :], MISC[:, l, 0:1], MISC[:, l, 2:3], ALU.mult, reads=["misc"], writes=["gs"])
        P.dma("sp", [(KRb[:, TPC:2 * TPC], KVL["kr"][:, :])], reads=[("kvloc", "kr", 0), ("kvloc", "kr", 1)], writes=["krB"])
        P.dma("sp", [(KRb[:, 0:TPC], KVA["kr"][0:128, :])], reads=[("kvall", "kr")], writes=["krA"])
        P.rot = {"main": [0, 1, 2], "aux": [3, 4], "stat": [5, 6, 7]}
        P.roti = {k: 0 for k in P.rot}
        for i in range(2):
            P.op("dve", lambda e, i=i: e.memset(QRb[i][64:128, :], 0.0), writes=[("QR", i)])
        pt_i = [0]
        ob_i = [0]

        def attend(qt, s_pieces_fn, s_reads, vbuf, v_res, scale):
            bo = P.bank("aux")
            bd = P.bank("stat")
            nkb = 8 + 4 * qt + 4
            LOOK = 2
            st = {}

            def issue_s(kb):
                kl = kb - 8
                c0 = max(0, 128 * (kl - 4 * qt)) if kl >= 0 else 0
                sb_ = P.bank("main")
                mm_group(P, P.ps[sb_][:, c0:TT], s_pieces_fn(kb, qt * TT + c0, TT - c0), reads=s_reads(kl >= 0), bank=sb_)
                i = pt_i[0]
                pt_i[0] = (i + 1) % NPT
                pt = PTb[i]
                if kl < 0:
                    act(P, pt[:, c0:TT], P.ps[sb_][:, c0:TT], AF.Exp, reads=[("ps", sb_), "cmask"], writes=[("pt", i)],
                        scale=scale, bias=CMASK[:, 0:1])
                else:
                    act(P, pt[:, c0:TT], P.ps[sb_][:, c0:TT], AF.Exp, reads=[("ps", sb_)], writes=[("pt", i)], scale=scale)
                    if kl >= 4 * qt:
                        P.op("dve", lambda e, pt=pt, c0=c0: e.memset(pt[64:128, c0:c0 + 64], 0.0), reads=[], writes=[("pt", i)])
                st[kb] = (i, pt, c0)

            def issue_pv(kb, first, last):
                i, pt, c0 = st.pop(kb)
                P.op("pe", lambda e, pt=pt, c0=c0, kb=kb, first=first, last=last, bo=bo: e.matmul(
                    P.ps[bo][:, c0:TT], vbuf[:, kb, :], pt[:, c0:TT], start=first, stop=last),
                    reads=[("pt", i)] + v_res(kb >= 8), writes=[("ps", bo)])
                P.op("pe", lambda e, pt=pt, c0=c0, first=first, last=last, bd=bd: e.matmul(
                    P.ps[bd][:, c0:TT], C.ONES[:, :], pt[:, c0:TT], start=first, stop=last),
                    reads=[("pt", i), "ones"], writes=[("ps", bd)])

            order = list(range(8, nkb)) + list(range(8))
            for idx in range(nkb + LOOK):
                if idx < nkb:
                    issue_s(order[idx])
                if idx - LOOK >= 0:
                    issue_pv(order[idx - LOOK], idx - LOOK == 0, idx - LOOK == nkb - 1)
            return bo, bd

        def astore(dst_ap, writer, tt):
            i = ob_i[0]
            ob_i[0] = (i + 1) % 2
            writer(AOB[i], ("aob", i))
            P.dma("sp", [(dst_ap, AOB[i])], reads=[("aob", i)], writes=[("otloc", tt)])

        def load_kv(i, kn_, vn_, h):
            krow = h * 128
            own = [(Kb[i][:, TPC:2 * TPC], KVL[kn_][krow:krow + 128, :]),
                   (Vb[i][:, 8:16, :], KVL[vn_][0:TPC, h * 128:(h + 1) * 128].rearrange("(kb p) d -> p kb d", p=128))]
            prev = [(Kb[i][:, 0:TPC], KVA[kn_][krow:krow + 128, :]),
                    (Vb[i][:, 0:8, :], KVA[vn_][0:TPC, h * 128:(h + 1) * 128].rearrange("(kb p) d -> p kb d", p=128))]
            P.dma("sp", own, reads=[("kvloc", n, t) for n in (kn_, vn_) for t in range(2)], writes=[("KB", i), ("VB", i)])
            P.dma("sp", prev, reads=[("kvall", kn_), ("kvall", vn_)], writes=[("KA", i), ("VA", i)])

        def load_mla(h):
            i = h % 2
            P.dma("sp", [(Qb[i], q_qn[h]), (QRb[i][0:64, :], q_qr[h // 2][(h % 2) * 64:(h % 2) * 64 + 64, :])],
                  reads=[("qloc", 0), ("qloc", 1)], writes=[("Q", i), ("QR", i)])
            load_kv(i, "kn", "vm", h)

        def kres(i, extra):
            return lambda own: [("Q", i), ("QR", i), ("KB" if own else "KA", i)] + [e + ("B" if own else "A") for e in extra]

        def vres(i):
            return lambda own: [("VB" if own else "VA", i)]

        load_mla(0)
        for h in range(8):
            i = h % 2
            if h + 1 < 8:
                load_mla(h + 1)
            for qt in range(2):
                def pieces(kb, q0, n, i=i):
                    return [(Kb[i][:, kb * 128:(kb + 1) * 128], Qb[i][:, q0:q0 + n]),
                            (KRb[:, kb * 128:(kb + 1) * 128], QRb[i][:, q0:q0 + n])]
                bo, bd = attend(qt, pieces, kres(i, ["kr"]), Vb[i], vres(i), MLA_SCALE)
                P.op("dve", lambda e, bd=bd, REC=REC: e.reciprocal(out=REC, in_=P.ps[bd][:, :]), reads=[("ps", bd)], writes=["rec"])
                astore(ot_loc[h][:, qt * TT:(qt + 1) * TT],
                       lambda ob, r, bo=bo: tt_op(P, "dve", ob, P.ps[bo][:, :], REC, ALU.mult, reads=[("ps", bo), "rec"], writes=[r]), qt)
        for i in range(2):
            P.op("dve", lambda e, i=i: e.memset(Qb[i][64:128, :], 0.0), writes=[("Q", i)])
            P.op("dve", lambda e, i=i: e.memset(QRb[i][0:64, :], 0.0), writes=[("QR", i)])
        def load_diff(h):
            i = h % 2
            P.dma("sp", [(Qb[i][0:64, :], q_dq[h][0:64, :]), (QRb[i][64:128, :], q_dq[h][64:128, :])],
                  reads=[("qloc", 0), ("qloc", 1)], writes=[("Q", i), ("QR", i)])
            load_kv(i, "dk", "vd", h)

        load_diff(0)
        pending_fin = [None]
        for h in range(8):
            i = h % 2
            if h + 1 < 8:
                load_diff(h + 1)
            for qt in range(2):
                for m in range(2):
                    def pieces(kb, q0, n, i=i, m=m):
                        qm = Qb[i] if m == 0 else QRb[i]
                        return [(Kb[i][:, kb * 128:(kb + 1) * 128], qm[:, q0:q0 + n])]
                    bo, bd = attend(qt, pieces, kres(i, []), Vb[i], vres(i), DIFF_SCALE)
                    if m == 0 and pending_fin[0] is not None:
                        pending_fin[0]()
                        pending_fin[0] = None
                    P.op("dve", lambda e, bd=bd, REC=REC: e.reciprocal(out=REC, in_=P.ps[bd][:, :]), reads=[("ps", bd)], writes=["rec"])
                    dst = A1 if m == 0 else A2
                    tt_op(P, "dve", dst, P.ps[bo][:, :], REC, ALU.mult, reads=[("ps", bo), "rec"], writes=["a%d" % m])
                od = ODb[(2 * h + qt) % 2]
                odr = ("od", (2 * h + qt) % 2)
                stt(P, od, A2, NEGLAM[:, 0:1], A1, ALU.mult, ALU.add, reads=["a0", "a1", "neglam"], writes=[odr])

                def fin(od=od, odr=odr, h=h, qt=qt):
                    rms_rstd(P, C, [(od, [odr])], TT, 128, RSTD[:, :], "rstd")
                    astore(ot_loc[8 + h][:, qt * TT:(qt + 1) * TT],
                           lambda ob, r: stt(P, ob, od, GS[:, 0:1], RSTD[:, :], ALU.mult, ALU.mult, reads=[odr, "gs", "rstd"], writes=[r]), qt)

                pending_fin[0] = fin
        pending_fin[0]()

        P.barrier()
        P.rot = {"main": [0, 1, 2, 3], "aux": [4, 5], "stat": [6, 7]}
        P.roti = {k: 0 for k in P.rot}
        ABUF = R2[:, :].bitcast(BF16).rearrange("p (c t) -> p c t", c=NFF)
        USB = [[R1[:, (2 * i + j) * 520:(2 * i + j) * 520 + TT + 2] for j in range(2)] for i in range(2)]
        YC = [R1[:, 2080 + j * 512:2592 + j * 512] for j in range(2)]
        GEL = R1[:, 3104:3616]
        PT = R2[:, 0:512].bitcast(BF16).rearrange("p (c t) -> p c t", c=2)
        P.dma("sp", [(CW[:, :, :], cw_in[l]), (CB[:, :], cb_in[l])], writes=["cw"])
        def halo_exchange(l=l):
            rms_rstd(P, C, [(H[:, c, TPC - 2:TPC], [("H", c, 1)]) for c in range(NCH)], 2, D, RSTDH[:, :], "rstdh")
            for c in range(NCH):
                stt(P, HNL[:, c, :], H[:, c, TPC - 2:TPC], GV[:, l, 1, c:c + 1], RSTDH[:, :], ALU.mult, ALU.mult,
                    reads=[("H", c, 1), "rstdh", "gvec"], writes=["hnl"])
            P.dma("sp", [(hl_loc, HNL[:, :, :].rearrange("p c t -> p (c t)"))], reads=["hnl"], writes=["hlloc"])
            P.collective(hl_loc_t, hl_all_t, reads=["hlloc"], writes=["hlall"])
            P.dma("sp", [(HNR[:, :, :].rearrange("p c t -> p (c t)"), hl_all[0:128, :])], reads=["hlall"], writes=["hnr"])
            ts(P, "dve", HNH[0][:, :, :], HNR[:, :, :], CMASK[:, 1:2], None, ALU.mult, ALU.bypass, reads=["hnr", "cmask"], writes=[("hnh", 0)])

        for tt in (1, 0):
            t0 = tt * TT
            P.dma("sp", [(ACTB[:, :, :], ot_loc[:, :, t0:t0 + TT].rearrange("c p t -> p c t"))], reads=[("otloc", tt)], writes=actb_all)

            def evac_wo(oc, bank, s, res, wv):
                act(P, F32T[:, oc, :], P.ps[bank][:, :], AF.Copy, reads=[("ps", bank)], writes=[("f32t", oc)])

            linear([w_o[l, oc] for oc in range(NCH)], NCH, lambda kc: ACTB[:, kc, :], actb_all, evac_wo)
            rms_rstd(P, C, [(F32T[:, c, :], [("f32t", c)]) for c in range(NCH)], TT, D, RSTD[:, :], "rstd")
            resid_add(tt, 0)
            if tt == 1:
                halo_exchange()
        for tt in range(2):
            t0 = tt * TT
            prenorm_to_actb(tt, 1)
            hnh = HNH[tt]
            for c in range(NFF):
                par = c % 2
                for j, tile_idx in enumerate((c, NFF + c)):
                    s, res = wload(w_up[l, tile_idx], NCH * 128)
                    wv = wslot(s)
                    bank = P.bank("main")
                    mm_group(P, P.ps[bank][:, :], [(wv[:, kc * 128:(kc + 1) * 128], ACTB[:, kc, :]) for kc in range(NCH)],
                             reads=res + actb_all, bank=bank)
                    u = USB[par][j]
                    ur = ("usb", par, j)
                    if tt == 0:
                        hb = P.bank("aux")
                        mm_group(P, P.ps[hb][:, 0:2], [(wv[:, kc * 128:(kc + 1) * 128], hnh[:, kc, :]) for kc in range(NCH)],
                                 reads=res + [("hnh", tt)], bank=hb)
                    act(P, u[:, 2:TT + 2], P.ps[bank][:, :], AF.Copy, reads=[("ps", bank)], writes=[ur])
                    if tt == 0:
                        act(P, u[:, 0:2], P.ps[hb][:, 0:2], AF.Copy, reads=[("ps", hb)], writes=[ur])
                        act(P, UH[:, tile_idx, :], u[:, TT:TT + 2], AF.Copy, reads=[ur], writes=[("uh", tile_idx)])
                    else:
                        act(P, u[:, 0:2], UH[:, tile_idx, :], AF.Copy, reads=[("uh", tile_idx)], writes=[ur])
                    y = YC[j]
                    yr = ("yc", j)
                    ts(P, "dve", y, u[:, 2:TT + 2], CW[:, tile_idx, 2:3], CB[:, tile_idx:tile_idx + 1], ALU.mult, ALU.add,
                       reads=[ur, "cw"], writes=[yr])
                    stt(P, y, u[:, 1:TT + 1], CW[:, tile_idx, 1:2], y, ALU.mult, ALU.add, reads=[ur, "cw", yr], writes=[yr])
                    stt(P, y, u[:, 0:TT], CW[:, tile_idx, 0:1], y, ALU.mult, ALU.add, reads=[ur, "cw", yr], writes=[yr])
                act(P, GEL, YC[0], AF.Gelu_apprx_tanh, reads=[("yc", 0)], writes=["gel"])
                tt_op(P, "dve", ABUF[:, c, :], GEL, YC[1], ALU.mult, reads=["gel", ("yc", 1)], writes=[("abuf", c)])

            def evac_dn(oc, bank, s, res, wv):
                act(P, F32T[:, oc, :], P.ps[bank][:, :], AF.Copy, reads=[("ps", bank)], writes=[("f32t", oc)])

            linear([w_dn[l, oc] for oc in range(NCH)], NFF, lambda kc: ABUF[:, kc, :], abuf_all, evac_dn)
            rms_rstd(P, C, [(F32T[:, c, :], [("f32t", c)]) for c in range(NCH)], TT, D, RSTD[:, :], "rstd")
            resid_add(tt, 2)
            for c in range(NCH):
                act(P, ACTB[:, c, :], Hc(c, tt), AF.Copy, reads=[("H", c, tt)], writes=[("actb", c)])
            P.dma("pool", [(PT, pT_in[l, :, :, t0:t0 + TT])], reads=abuf_all, writes=["pt_ple"] + abuf_all)
            for oc in range(NCH):
                s, res = wload(w_pg[l, oc], NCH * 128)
                wv = wslot(s)
                b1 = P.bank("main")
                mm_group(P, P.ps[b1][:, :], [(wv[:, kc * 128:(kc + 1) * 128], ACTB[:, kc, :]) for kc in range(NCH)],
                         reads=res + actb_all, bank=b1)
                s2, res2 = wload(w_pl[l, oc], 2 * 128)
                wv2 = wslot(s2)
                b2 = P.bank("main")
                mm_group(P, P.ps[b2][:, :], [(wv2[:, kc * 128:(kc + 1) * 128], PT[:, kc, :]) for kc in range(2)],
                         reads=res2 + ["pt_ple"], bank=b2)
                act(P, SG[:, :], P.ps[b1][:, :], AF.Sigmoid, reads=[("ps", b1)], writes=["sg"])
                tt_op(P, "dve", F32T[:, oc, :], SG[:, :], P.ps[b2][:, :], ALU.mult, reads=["sg", ("ps", b2)], writes=[("f32t", oc)])
            rms_rstd(P, C, [(F32T[:, c, :], [("f32t", c)]) for c in range(NCH)], TT, D, RSTD[:, :], "rstd")
            resid_add(tt, 3)

    for tt in range(2):
        P.dma("sp", [(hT_out[:, :, tt * TT:(tt + 1) * TT], H[:, :, tt * TT:(tt + 1) * TT])],
              reads=[("H", c, tt) for c in range(NCH)], is_out=True)
    return P


def tile_w(W, cols, KC):
    Wc = W[:, cols]
    M = Wc.shape[1]
    return np.ascontiguousarray(Wc.reshape(KC, 128, M).transpose(1, 0, 2).reshape(128, KC * M))


def tile_w_blocks(W, KC, nblk, bw=128):
    return np.ascontiguousarray(W.reshape(KC, 128, nblk, bw).transpose(2, 1, 0, 3).reshape(nblk, 128, KC * bw))


def fm_vec(g):
    return np.ascontiguousarray(g.reshape(-1, 128).T)


def to_fm(x):
    T, Fd = x.shape
    return np.ascontiguousarray(x.T.reshape(Fd // 128, 128, T).transpose(1, 0, 2))


def from_fm(xf):
    _, Cn, T = xf.shape
    return np.ascontiguousarray(xf.transpose(1, 0, 2).reshape(Cn * 128, T).T)


def const_tables():
    invf = np.zeros((128, 2), np.float32)
    pm = np.zeros((2, 128, 128), np.float32)
    for p in range(128):
        r = p % 64
        invf[p, 0] = THETA ** (-(2.0 * (r % 32)) / 64.0)
        if r < 32:
            pm[0, p + 32, p] = -1.0
        else:
            pm[0, p - 32, p] = 1.0
        if r < 16:
            invf[p, 1] = THETA ** (-(2.0 * (r % 8)) / 16.0)
            if r < 8:
                pm[1, p + 8, p] = -1.0
            else:
                pm[1, p - 8, p] = 1.0
    return invf, pm


def shared_inputs(w, L=DEPTH):
    m = {}
    fm_groups = [list(range(c * 128, (c + 1) * 128)) for c in range(6)]
    fm_groups.append(list(range(768, 832)) * 2)
    fm_groups += [list(range(832 + c * 128, 832 + (c + 1) * 128)) for c in range(16)]
    gq = [list(range(h * 192, h * 192 + 128)) for h in range(8)]
    gq += [list(range((2 * r) * 192 + 128, (2 * r) * 192 + 192)) + list(range((2 * r + 1) * 192 + 128, (2 * r + 1) * 192 + 192))
           for r in range(4)]
    vcols = [h * 256 + 128 + j for h in range(8) for j in range(128)]
    m["w_in_fm"] = np.stack([np.stack([tile_w(w["w_in"][l], g, 16) for g in fm_groups]) for l in range(L)])
    m["w_in_dv"] = np.stack([tile_w_blocks(w["w_in"][l][:, 2880:3904], 16, 2, 512) for l in range(L)])
    m["w_uq"] = np.stack([np.stack([tile_w(w["w_uq"][l], g, 4) for g in gq]) for l in range(L)])
    m["w_ukv_k"] = np.stack([np.stack([tile_w(w["w_ukv"][l], list(range(h * 256, h * 256 + 128)), 2) for h in range(8)]) for l in range(L)])
    m["w_ukv_v"] = np.stack([tile_w(w["w_ukv"][l], vcols, 2) for l in range(L)])
    m["w_o"] = np.stack([tile_w_blocks(w["w_o"][l], 16, 16) for l in range(L)])
    m["w_up"] = np.stack([tile_w_blocks(w["w_up"][l], 16, 2 * NFF) for l in range(L)])
    m["conv_w"] = np.stack([np.ascontiguousarray(w["conv_w"][l].reshape(3, 2 * NFF, 128).transpose(2, 1, 0)) for l in range(L)])
    m["conv_b"] = np.stack([fm_vec(w["conv_b"][l]) for l in range(L)])
    m["w_down"] = np.stack([tile_w_blocks(w["w_down"][l], NFF, 16) for l in range(L)])
    m["w_ple_gate"] = np.stack([tile_w_blocks(w["w_ple_gate"][l], 16, 16) for l in range(L)])
    m["w_ple"] = np.stack([tile_w_blocks(w["w_ple"][l], 2, 16) for l in range(L)])
    g = np.zeros((128, L, 8, NCH), np.float32)
    for l in range(L):
        for r, k in enumerate(("g_mix_post", "g_ffn_pre", "g_ffn_post", "g_ple", "g_mix_pre")):
            g[:, l, r, :] = fm_vec(w[k][l])
        g[:, l, 5, :4] = fm_vec(w["g_q_lora"][l])
        g[:, l, 6, :2] = fm_vec(w["g_kv_lora"][l])
    m["gvec"] = g
    lamv = np.stack([np.stack([w[k][l] for k in ("lambda_q1", "lambda_k1", "lambda_q2", "lambda_k2")]) for l in range(L)])
    m["lamv"] = np.ascontiguousarray(np.broadcast_to(lamv[None], (128, L, 4, 64))).astype(np.float32)
    misc = np.zeros((128, L, 4), np.float32)
    for l in range(L):
        linit = 0.8 - 0.6 * math.exp(-0.3 * l)
        misc[:, l, 0] = w["g_diff_sub"][l]
        misc[:, l, 1] = linit
        misc[:, l, 2] = 1.0 - linit
    m["misc"] = misc
    invf, pm = const_tables()
    m["invf"] = invf
    m["pm"] = pm
    return m


def core_inputs(w, core, L=DEPTH):
    b, half = core // 2, core % 2
    t0 = half * TPC
    m = {}
    m["hT"] = to_fm(w["x"][b, t0:t0 + TPC])
    pos = w["positions"][b, t0:t0 + TPC].astype(np.int32)
    m["pos"] = np.ascontiguousarray(np.broadcast_to(pos[None, :], (128, TPC)))
    m["pT"] = np.stack([to_fm(w["p"][l, b, t0:t0 + TPC]) for l in range(L)])
    cm = np.zeros((128, 2), np.float32)
    cm[:, 0] = 0.0 if half == 1 else -200.0
    cm[:, 1] = 1.0 if half == 1 else 0.0
    m["cmask"] = cm
    return m


_PROG = {}


def get_prog(L=DEPTH):
    if L not in _PROG:
        P = build_fused(L)
        P.finish()
        _PROG[L] = P
    return _PROG[L]


def kernel(**inputs):
    w = {k: np.asarray(v) for k, v in inputs.items()}
    P = get_prog(DEPTH)
    sh = shared_inputs(w)
    maps = []
    for c in range(8):
        m = dict(sh)
        m.update(core_inputs(w, c))
        maps.append(m)
    res = run_bass_kernel_spmd(P.nc, maps, core_ids=list(range(8))).results
    out = np.zeros((B, S, D), np.float32)
    for c in range(8):
        b, half = c // 2, c % 2
        out[b, half * TPC:(half + 1) * TPC] = from_fm(np.asarray(res[c]["hT_out"]))
    return out
```
